# Optimizing a Trainium2 kernel written in Bass

```python
import jax, jax.numpy as jnp
from jax import lax
import numpy as np

D_MODEL = 2048
BATCH = 4
SEQ = 2048
DEPTH = 4
DEC_BATCH = 8
DEC_SEQ = 8
PAST_LEN = 16384
PAGE_SIZE = 128

N_MIXERS = 2
N_CONV_LAYERS = (DEPTH + 1) // 2
N_ATTN_LAYERS = DEPTH // 2
CONV_WIDTH = 31
CONV_STATE = CONV_WIDTH - 1
WINDOWS = (128, 512, 2048)
DILATIONS = (1, 4, 16)
N_GROUPS = 3
HEAD_DIM = 128
HEADS_PER_GROUP = D_MODEL // (2 * HEAD_DIM)
GROUP_WIDTH = HEADS_PER_GROUP * HEAD_DIM
QKV_WIDTH = N_GROUPS * 3 * GROUP_WIDTH
D_FF = 4 * D_MODEL
RMS_EPS = 1e-6
LN_EPS = 1e-5
NEG_INF = -1e30
SCALE = HEAD_DIM ** -0.5

kernel_name = "conformer_conv_dilated_swa_hybrid_step"


def rms_norm(x, g):
    xf = x.astype(jnp.float32)
    y = xf * lax.rsqrt(jnp.mean(xf * xf, axis=-1, keepdims=True) + RMS_EPS)
    return (y * g.astype(jnp.float32)).astype(x.dtype)


def sqrelu_mlp(h, w1, w2):
    a = jnp.maximum(h @ w1, 0)
    return (a * a) @ w2


def conv_module(h, conv_buf, w_pw1, b_pw1, w_dw, b_dw, ln_g, ln_b, w_pw2, b_pw2):
    u = h @ w_pw1 + b_pw1
    a, gate = jnp.split(u, 2, axis=-1)
    glu = a * jax.nn.sigmoid(gate)
    padded = jnp.concatenate([conv_buf.astype(glu.dtype), glu], axis=1)
    c = lax.conv_general_dilated(
        padded, w_dw[:, None, :].astype(padded.dtype), window_strides=(1,),
        padding='VALID', dimension_numbers=('NWC', 'WIO', 'NWC'),
        feature_group_count=D_MODEL) + b_dw
    cf = c.astype(jnp.float32)
    mu = jnp.mean(cf, axis=-1, keepdims=True)
    var = jnp.mean(jnp.square(cf - mu), axis=-1, keepdims=True)
    n = (cf - mu) * lax.rsqrt(var + LN_EPS) * ln_g.astype(jnp.float32) + ln_b.astype(jnp.float32)
    s = (n * jax.nn.sigmoid(n)).astype(h.dtype)
    out = s @ w_pw2 + b_pw2
    return out, padded[:, -CONV_STATE:]


def dilated_group_prompt(q, k, v, window, dilation):
    B, T, H, E = q.shape
    band = window // dilation
    span = band * dilation
    t_pad = -(-T // span) * span
    nb = t_pad // span
    pad = ((0, 0), (0, t_pad - T), (0, 0), (0, 0))

    def to_blocks(a):
        return jnp.pad(a, pad).reshape(B, nb, band, dilation, H, E)

    def banded(a):
        ap = jnp.pad(a, ((0, 0), (1, 0), (0, 0), (0, 0), (0, 0), (0, 0)))
        return jnp.concatenate([ap[:, :-1], ap[:, 1:]], axis=2)

    qb = to_blocks(q)
    kk = banded(to_blocks(k))
    vv = banded(to_blocks(v))
    s = jnp.einsum('bnqrhe,bnkrhe->bhrnqk', qb, kk, preferred_element_type=jnp.float32)
    qi = jnp.arange(band)[:, None]
    ki = jnp.arange(2 * band)[None, :]
    dist = band + qi - ki
    blk = jnp.arange(nb)[:, None, None]
    valid = (dist >= 0) & (dist <= band) & (blk * band + ki - band >= 0)
    s = jnp.where(valid, s, NEG_INF)
    lse = jax.nn.logsumexp(s, axis=-1, keepdims=True)
    p = jnp.exp(s - lse)
    o = jnp.einsum('bhrnqk,bnkrhe->bnqrhe', p.astype(vv.dtype), vv,
                   preferred_element_type=jnp.float32)
    o = o.reshape(B, t_pad, H, E)[:, :T]
    lse = lse[..., 0].transpose(0, 3, 4, 2, 1).reshape(B, t_pad, H)[:, :T]
    return o, lse


def dilated_group_sample(q, k_cat, v_cat, n_past, window, dilation):
    S = q.shape[1]
    n_keys = window // dilation + 1
    idx = n_past + jnp.arange(S)[:, None] - dilation * jnp.arange(n_keys)[None, :]
    valid = idx >= 0
    idx_c = jnp.maximum(idx, 0)
    kg = k_cat[:, idx_c]
    vg = v_cat[:, idx_c]
    s = jnp.einsum('bshe,bsnhe->bhsn', q, kg, preferred_element_type=jnp.float32)
    s = jnp.where(valid, s, NEG_INF)
    lse = jax.nn.logsumexp(s, axis=-1, keepdims=True)
    p = jnp.exp(s - lse)
    o = jnp.einsum('bhsn,bsnhe->bshe', p.astype(vg.dtype), vg,
                   preferred_element_type=jnp.float32)
    return o, lse[..., 0].transpose(0, 2, 1)


def merge_groups(outs, lses, dtype):
    w = jax.nn.softmax(jnp.stack(lses), axis=0)
    return jnp.sum(w[..., None] * jnp.stack(outs), axis=0).astype(dtype)


def split_qkv(h, w_qkv):
    B, T, _ = h.shape
    return (h @ w_qkv).reshape(B, T, N_GROUPS, 3, HEADS_PER_GROUP, HEAD_DIM)


def attn_prompt(h, w_qkv, w_o):
    B, T, _ = h.shape
    qkv = split_qkv(h, w_qkv)
    outs, lses, new_kv = [], [], []
    for g in range(N_GROUPS):
        q = qkv[:, :, g, 0] * SCALE
        k = qkv[:, :, g, 1]
        v = qkv[:, :, g, 2]
        o, l = dilated_group_prompt(q, k, v, WINDOWS[g], DILATIONS[g])
        outs.append(o)
        lses.append(l)
        keep = min(WINDOWS[g], T)
        new_kv.append(jnp.stack([k[:, T - keep:], v[:, T - keep:]], axis=2))
    o = merge_groups(outs, lses, h.dtype).reshape(B, T, GROUP_WIDTH)
    return o @ w_o, new_kv


def attn_sample(h, kv_bufs, w_qkv, w_o):
    B, S, _ = h.shape
    qkv = split_qkv(h, w_qkv)
    outs, lses, new_kv = [], [], []
    for g in range(N_GROUPS):
        buf = kv_bufs[g]
        q = qkv[:, :, g, 0] * SCALE
        k = qkv[:, :, g, 1]
        v = qkv[:, :, g, 2]
        k_cat = jnp.concatenate([buf[:, :, 0].astype(k.dtype), k], axis=1)
        v_cat = jnp.concatenate([buf[:, :, 1].astype(v.dtype), v], axis=1)
        o, l = dilated_group_sample(q, k_cat, v_cat, buf.shape[1], WINDOWS[g], DILATIONS[g])
        outs.append(o)
        lses.append(l)
        new_kv.append(jnp.stack([k, v], axis=2))
    o = merge_groups(outs, lses, h.dtype).reshape(B, S, GROUP_WIDTH)
    return o @ w_o, new_kv


def setup_inputs(seed: int = 0) -> dict:
    key = jax.random.key(seed)
    ks = jax.random.split(key, 24)
    f32 = jnp.float32
    nrm = lambda k, shape, scale=1.0: jax.random.normal(k, shape, f32) * scale
    cache_lens = [min(w, PAST_LEN) for w in WINDOWS]
    kv_shape = lambda L: (N_ATTN_LAYERS, DEC_BATCH, L, 2, HEADS_PER_GROUP, HEAD_DIM)
    return {
        'x_prompt': nrm(ks[0], (BATCH, SEQ, D_MODEL)),
        'x_sample': nrm(ks[1], (DEC_BATCH, DEC_SEQ, D_MODEL)),
        'state_conv': nrm(ks[2], (N_CONV_LAYERS, DEC_BATCH, CONV_STATE, D_MODEL), 0.5),
        'cache_kv_w128': nrm(ks[3], kv_shape(cache_lens[0])),
        'cache_kv_w512': nrm(ks[4], kv_shape(cache_lens[1])),
        'cache_kv_w2048': nrm(ks[5], kv_shape(cache_lens[2])),
        'norm_mix': 1.0 + nrm(ks[6], (DEPTH, D_MODEL), 0.01),
        'norm_mlp': 1.0 + nrm(ks[7], (DEPTH, D_MODEL), 0.01),
        'norm_final': 1.0 + nrm(ks[8], (D_MODEL,), 0.01),
        'conv_w_pw1': nrm(ks[9], (N_CONV_LAYERS, D_MODEL, 2 * D_MODEL), D_MODEL ** -0.5),
        'conv_b_pw1': nrm(ks[10], (N_CONV_LAYERS, 2 * D_MODEL), 0.01),
        'conv_w_dw': nrm(ks[11], (N_CONV_LAYERS, CONV_WIDTH, D_MODEL), CONV_WIDTH ** -0.5),
        'conv_b_dw': nrm(ks[12], (N_CONV_LAYERS, D_MODEL), 0.01),
        'conv_ln_g': 1.0 + nrm(ks[13], (N_CONV_LAYERS, D_MODEL), 0.01),
        'conv_ln_b': nrm(ks[14], (N_CONV_LAYERS, D_MODEL), 0.01),
        'conv_w_pw2': nrm(ks[15], (N_CONV_LAYERS, D_MODEL, D_MODEL), D_MODEL ** -0.5),
        'conv_b_pw2': nrm(ks[16], (N_CONV_LAYERS, D_MODEL), 0.01),
        'attn_w_qkv': nrm(ks[17], (N_ATTN_LAYERS, D_MODEL, QKV_WIDTH), D_MODEL ** -0.5),
        'attn_w_o': nrm(ks[18], (N_ATTN_LAYERS, GROUP_WIDTH, D_MODEL), GROUP_WIDTH ** -0.5),
        'mlp_w1': nrm(ks[19], (DEPTH, D_MODEL, D_FF), D_MODEL ** -0.5),
        'mlp_w2': nrm(ks[20], (DEPTH, D_FF, D_MODEL), D_FF ** -0.5),
    }


def reference(x_prompt, x_sample, state_conv, cache_kv_w128, cache_kv_w512, cache_kv_w2048,
              norm_mix, norm_mlp, norm_final,
              conv_w_pw1, conv_b_pw1, conv_w_dw, conv_b_dw, conv_ln_g, conv_ln_b,
              conv_w_pw2, conv_b_pw2, attn_w_qkv, attn_w_o, mlp_w1, mlp_w2):
    yp, ys = x_prompt, x_sample
    conv_p, conv_s = [], []
    kv_p = [[] for _ in range(N_GROUPS)]
    kv_s = [[] for _ in range(N_GROUPS)]
    for i in range(DEPTH):
        j = i // N_MIXERS
        hp = rms_norm(yp, norm_mix[i])
        hs = rms_norm(ys, norm_mix[i])
        if i % N_MIXERS == 0:
            params = (conv_w_pw1[j], conv_b_pw1[j], conv_w_dw[j], conv_b_dw[j],
                      conv_ln_g[j], conv_ln_b[j], conv_w_pw2[j], conv_b_pw2[j])
            zero_buf = jnp.zeros((hp.shape[0], CONV_STATE, D_MODEL), hp.dtype)
            mp, stp = conv_module(hp, zero_buf, *params)
            ms, sts = conv_module(hs, state_conv[j], *params)
            conv_p.append(stp)
            conv_s.append(sts)
        else:
            bufs = (cache_kv_w128[j], cache_kv_w512[j], cache_kv_w2048[j])
            mp, nkp = attn_prompt(hp, attn_w_qkv[j], attn_w_o[j])
            ms, nks = attn_sample(hs, bufs, attn_w_qkv[j], attn_w_o[j])
            for g in range(N_GROUPS):
                kv_p[g].append(nkp[g])
                kv_s[g].append(nks[g])
        yp = yp + mp
        ys = ys + ms
        yp = yp + sqrelu_mlp(rms_norm(yp, norm_mlp[i]), mlp_w1[i], mlp_w2[i])
        ys = ys + sqrelu_mlp(rms_norm(ys, norm_mlp[i]), mlp_w1[i], mlp_w2[i])
    y_prompt = rms_norm(yp, norm_final)
    y_sample = rms_norm(ys, norm_final)
    return (y_prompt, y_sample,
            jnp.stack(conv_p), jnp.stack(conv_s),
            jnp.stack(kv_p[0]), jnp.stack(kv_s[0]),
            jnp.stack(kv_p[1]), jnp.stack(kv_s[1]),
            jnp.stack(kv_p[2]), jnp.stack(kv_s[2]))
```

```python
import numpy as np
import concourse.bass as bass
import concourse.mybir as mybir
from concourse.bass_utils import run_bass_kernel_spmd

F32 = mybir.dt.float32
BF16 = mybir.dt.bfloat16
AF = mybir.ActivationFunctionType
ALU = mybir.AluOpType

D = 2048
KC = 16
NP_ = 1024
NS = 8
NT = 1032
HALO = 30
NH = NT + HALO
DFF = 8192
RMS_EPS = 1e-6
LN_EPS = 1e-5
SCALE = 128 ** -0.5
NEG = -30000.0
TILES = [(0, 344), (344, 344), (688, 344)]
CTILES = [(0, 354), (354, 354), (708, 354)]

P_NMIX, P_NMLP, P_NFIN = 0, 64, 128
P_CONV = 144
PC_BA, PC_BG, PC_WDW, PC_BDW, PC_LNG, PC_LNB, PC_BPW2, PC_SIZE = 0, 16, 32, 528, 544, 560, 576, 592
NPAR = P_CONV + 2 * PC_SIZE

XK = 0
XV = 3 * 8 * NT
KVX = XV


class Res:
    __slots__ = ("w", "r")

    def __init__(self):
        self.w = None
        self.r = {}


class DmaSem:
    def __init__(self, sem):
        self.sem = sem
        self.n = 0
        self.unit = 16


class Eng:
    def __init__(self, name, sem):
        self.name = name
        self.sem = sem
        self.n = 0
        self.ins = []
        self.known = {}


class Prog:
    def __init__(self, nc, es):
        self.nc = nc
        self.es = es
        self.E = {}
        for nm in ("pe", "act", "dve", "pool", "sp"):
            self.E[nm] = Eng(nm, es.enter_context(nc.semaphore("s_" + nm)))
        self.nsem = 0
        self.dsems = []

    def dsem(self):
        self.nsem += 1
        d = DmaSem(self.es.enter_context(self.nc.semaphore("d%d" % self.nsem)))
        self.dsems.append(d)
        return d

    def _deps(self, eng, reads, writes):
        need = {}

        def add(tok, same_ok):
            if tok is None:
                return
            sem, val, src = tok
            if same_ok and src is eng:
                return
            k = id(sem)
            if k not in need or need[k][1] < val:
                need[k] = (sem, val)

        for r in reads:
            add(r.w, False)
        for w in writes:
            add(w.w, True)
            for t in w.r.values():
                add(t, True)
        out = []
        for k, (sem, val) in need.items():
            if eng.known.get(k, 0) < val:
                eng.known[k] = val
                out.append((sem, val))
        return out

    def _mark(self, tok, reads, writes):
        for r in reads:
            k = id(tok[0])
            r.r[k] = tok
        for w in writes:
            w.w = tok
            w.r = {}

    def op(self, en, fn, reads=(), writes=()):
        eng = self.E[en]
        waits = self._deps(eng, reads, writes)
        eng.n += 1
        n = eng.n
        sem = eng.sem

        def run(e):
            for s, v in waits:
                e.wait_ge(s, v)
            fn(e).then_inc(sem, 1)

        eng.ins.append(run)
        self._mark((sem, n, eng), reads, writes)

    def group(self, fns, reads=(), writes=()):
        eng = self.E["pe"]
        waits = self._deps(eng, reads, writes)
        eng.n += 1
        sem = eng.sem

        def run(e):
            for s, v in waits:
                e.wait_ge(s, v)
            last = None
            for f in fns:
                last = f(e)
            last.then_inc(sem, 1)

        eng.ins.append(run)
        self._mark((sem, eng.n, eng), reads, writes)

    def dma(self, en, out, in_, ds, reads=(), writes=()):
        eng = self.E[en]
        waits = self._deps(eng, reads, writes)
        ds.n += 1
        val = ds.n * 16
        sem = ds.sem

        def run(e):
            for s, v in waits:
                e.wait_ge(s, v)
            e.dma_start(out=out, in_=in_).then_inc(sem, 16)

        eng.ins.append(run)
        self._mark((sem, val, None), reads, writes)

    def coll(self, ins, outs, ds, reads=(), writes=()):
        eng = self.E["pool"]
        waits = self._deps(eng, reads, writes)
        ds.n += 1
        assert ds.n == 1
        ds.unit = 1
        sem = ds.sem

        def run(e):
            for s, v in waits:
                e.wait_ge(s, v)
            e.collective_compute("AllGather", ALU.bypass,
                                 replica_groups=[[0, 1], [2, 3], [4, 5], [6, 7]],
                                 ins=ins, outs=outs).then_inc(sem)

        eng.ins.append(run)
        self._mark((sem, 1, None), reads, writes)

    SEM_ROTATE_AT = 1200

    def barrier(self):
        for eng in self.E.values():
            waits = []
            for o in self.E.values():
                if o.n == 0:
                    continue
                if o is eng and eng.name == "sp":
                    continue
                k = id(o.sem)
                if eng.known.get(k, 0) < o.n:
                    eng.known[k] = o.n
                    waits.append((o.sem, o.n))
            for d in self.dsems:
                k = id(d.sem)
                if d.n and eng.known.get(k, 0) < d.n * d.unit:
                    eng.known[k] = d.n * d.unit
                    waits.append((d.sem, d.n * d.unit))
            if waits:
                def run(e, waits=waits):
                    for s, v in waits:
                        e.wait_ge(s, v)
                eng.ins.append(run)
        for o in self.E.values():
            if o.n > self.SEM_ROTATE_AT:
                old = id(o.sem)
                for eng in self.E.values():
                    eng.known[old] = 1 << 40
                self.nsem += 1
                o.sem = self.es.enter_context(self.nc.semaphore("r%d_%s" % (self.nsem, o.name)))
                o.n = 0

    def final_wait(self, dsems):
        eng = self.E["sp"]
        waits = [(d.sem, d.n * d.unit) for d in dsems if d.n > 0]
        waits += [(o.sem, o.n) for o in self.E.values() if o is not eng and o.n > 0]

        def run(e):
            for s, v in waits:
                e.wait_ge(s, v)

        eng.ins.append(run)

    def emit(self, block):
        E = self.E

        @block.tensor
        def _(e):
            for f in E["pe"].ins:
                f(e)

        @block.scalar
        def _(e):
            for f in E["act"].ins:
                f(e)

        @block.vector
        def _(e):
            for f in E["dve"].ins:
                f(e)

        @block.gpsimd
        def _(e):
            for f in E["pool"].ins:
                f(e)

        @block.sync
        def _(e):
            for f in E["sp"].ins:
                f(e)


def build(stage=99, stop=None):
    from contextlib import ExitStack
    nc = bass.Bass("TRN2", target_bir_lowering=False)
    es = ExitStack()

    def din(name, shape):
        return nc.dram_tensor(name, list(shape), F32, kind="ExternalInput")

    def dout(name, shape):
        return nc.dram_tensor(name, list(shape), F32, kind="ExternalOutput")

    xT = din("xT", [D, NT])
    xh0 = din("xh0", [D, HALO])
    cs_in = din("cs_in", [2, D, HALO])
    par_d = din("par", [128, NPAR])
    flag_d = din("flag", [128, 1])
    mgen_d = din("mgen2", [128, 512])
    mfirst_d = din("mfirst2", [128, 512])
    mg2_d = din("mg2x8", [128, 512])
    msc_d = din("ms_cache", [128, 104])
    msn_d = din("ms_new", [128, 24])
    ckT = din("ckT", [2, 128, 8, 13 * 128])
    cv = din("cv", [2, 128, 13, 1024])
    w_pw1 = din("conv_w_pw1", [2, D, 2 * D])
    w_pw2 = din("conv_w_pw2", [2, D, D])
    w_qkv = din("attn_w_qkv", [2, D, 9216])
    w_o = din("attn_w_o", [2, 1024, D])
    w_1 = din("mlp_w1", [4, D, DFF])
    w_2 = din("mlp_w2", [4, DFF, D])

    yT = dout("yT", [D, NT])
    cs_p = dout("cs_p", [2, D, HALO])
    cs_s = dout("cs_s", [2, D, HALO])
    kT_out = dout("kT_out", [2, 3, 8, 128, NT])
    v_out = dout("v_out", [2, 3, NT, 1024])

    exk = [nc.dram_tensor("exk%d" % j, [128, KVX], BF16) for j in range(2)]
    exv = [nc.dram_tensor("exv%d" % j, [24 * NT, 128], BF16) for j in range(2)]
    NEED = [128, 512, 1024]
    sk = [[nc.dram_tensor("sk%d_%d" % (j, g), [128, 8 * NEED[g]], BF16) for g in range(3)] for j in range(2)]
    rk = [[nc.dram_tensor("rk%d_%d" % (j, g), [256, 8 * NEED[g]], BF16) for g in range(3)] for j in range(2)]
    sv = [[nc.dram_tensor("sv%d_%d" % (j, g), [128, 8 * NEED[g]], BF16) for g in range(3)] for j in range(2)]
    rv = [[nc.dram_tensor("rv%d_%d" % (j, g), [256, 8 * NEED[g]], BF16) for g in range(3)] for j in range(2)]
    xh_i = nc.dram_tensor("xh_i", [128, KC * HALO], F32)
    xh_o = nc.dram_tensor("xh_o", [256, KC * HALO], F32)

    sb = lambda name, shape, dt: es.enter_context(nc.sbuf_tensor(name, list(shape), dt))
    X = sb("X", [128, KC, NT], F32)
    H = sb("H", [128, KC, NH], BF16)
    W = [sb("W%d" % i, [128, KC, 256], BF16) for i in range(3)]
    PAR = sb("PAR", [128, NPAR], F32)
    ones = sb("ones", [128, 128], BF16)
    ident = sb("ident", [128, 128], BF16)
    identf = sb("identf", [128, 128], F32)
    epsr = sb("epsr", [128, 1], F32)
    epsl = sb("epsl", [128, 1], F32)
    flag = sb("flag_s", [128, 1], F32)
    Xh = sb("Xh", [128, KC, HALO], F32)
    CSI = sb("CSI", [128, KC, HALO], F32)
    ARENA_B = 77400
    AR = sb("AR", [128, ARENA_B // 2], BF16)
    ARf = AR.bitcast(F32)
    PS = [es.enter_context(nc.psum_tensor("ps%d" % i, [128, 512], F32)) for i in range(8)]

    def carve(off_bytes, cols, dt):
        if dt == F32:
            assert off_bytes % 4 == 0
            return ARf[:, off_bytes // 4: off_bytes // 4 + cols]
        assert off_bytes % 2 == 0
        return AR[:, off_bytes // 2: off_bytes // 2 + cols]

    p = Prog(nc, es)
    rX = [Res() for _ in range(KC)]
    rH = Res()
    rW = [Res() for _ in range(3)]
    dW = [p.dsem() for _ in range(3)]
    rPS = [Res() for _ in range(8)]
    rPAR = Res()
    rC = Res()
    dC = p.dsem()
    out_sems = []
    wctr = [0]
    pctr = [0]

    def next_w():
        i = wctr[0] % 3
        wctr[0] += 1
        return i

    def next_ps(lo=0, hi=6):
        i = lo + pctr[0] % (hi - lo)
        pctr[0] += 1
        return i

    def PARc(c, n=1):
        return PAR[:, c:c + n]

    p.dma("sp", PAR[:], par_d.ap(), dC, writes=[rPAR])
    p.dma("sp", flag[:], flag_d.ap(), dC, writes=[rC])
    xv = xT.ap().rearrange("(kc p) n -> p kc n", p=128)
    dX = p.dsem()
    for q in range(4):
        p.dma("sp", X[:, 4 * q:4 * q + 4, :], xv[:, 4 * q:4 * q + 4, :], dX, writes=rX[4 * q:4 * q + 4])
    rXh = Res()
    dXh = p.dsem()
    p.dma("sp", Xh[:], xh0.ap().rearrange("(kc p) n -> p kc n", p=128), dXh, writes=[rXh])
    p.op("pool", lambda e: e.memset(ones[:], 1.0), writes=[rC])
    p.op("pool", lambda e: e.memset(epsr[:], RMS_EPS), writes=[rC])
    p.op("pool", lambda e: e.memset(epsl[:], LN_EPS), writes=[rC])
    p.op("pool", lambda e: e.memset(identf[:], 0.0), writes=[rC])
    p.op("pool", lambda e: e.affine_select(out=identf[:], in_=identf[:], pattern=[[-1, 128]],
                                           compare_op=ALU.not_equal, fill=1.0, base=0, channel_multiplier=1),
         reads=[rC], writes=[rC])
    p.op("pool", lambda e: e.tensor_copy(out=ident[:], in_=identf[:]), reads=[rC], writes=[rC])

    def rmsnorm(gcol, with_halo, sq_off=0):
        SQ = carve(sq_off, KC * 354, BF16)
        rstd = carve(sq_off + KC * 354 * 2, NT, F32)
        sdt = carve(sq_off + KC * 354 * 2 + NT * 4, 354, F32)
        rSQ, rR, rS = Res(), Res(), Res()
        tl = [(X, c0, n, c0 + HALO, rX) for (c0, n) in TILES]
        if with_halo:
            tl.append((Xh, 0, HALO, 0, [rXh]))
        for (src, c0, n, h0, rsrc) in tl:
            sq = SQ.rearrange("p (k n) -> p k n", k=KC)[:, :, 0:n]
            p.op("act", lambda e, src=src, c0=c0, n=n, sq=sq: e.activation(out=sq, in_=src[:, :, c0:c0 + n], func=AF.Square),
                 reads=rsrc, writes=[rSQ])
            b = next_ps(6, 8)
            p.group([lambda e, kc=kc, b=b, n=n, sq=sq: e.matmul(PS[b][:, 0:n], lhsT=ones[:], rhs=sq[:, kc, :],
                                                               start=(kc == 0), stop=(kc == KC - 1), skip_group_check=True) for kc in range(KC)],
                    reads=[rSQ, rC], writes=[rPS[b]])
            p.op("act", lambda e, b=b, n=n: e.activation(out=sdt[:, 0:n], in_=PS[b][:, 0:n], func=AF.Sqrt,
                                                         bias=epsr[:], scale=1.0 / D),
                 reads=[rPS[b], rC], writes=[rS])
            rs = rstd[:, 0:n]
            p.op("dve", lambda e, n=n, rs=rs: e.reciprocal(out=rs, in_=sdt[:, 0:n]), reads=[rS], writes=[rR])
            for kc in range(KC):
                p.op("dve", lambda e, kc=kc, src=src, c0=c0, n=n, h0=h0, rs=rs: e.scalar_tensor_tensor(
                    out=H[:, kc, h0:h0 + n], in0=src[:, kc, c0:c0 + n], scalar=PARc(gcol + kc), in1=rs,
                    op0=ALU.mult, op1=ALU.mult),
                     reads=[rR, rPAR] + (rsrc if len(rsrc) == 1 else [rsrc[kc]]), writes=[rH])

    def load_stripe(wview, r0, nk, c0, ncols, extra=None):
        i = next_w()
        src = wview[r0:r0 + nk * 128, c0:c0 + ncols].rearrange("(kc p) n -> p kc n", p=128)
        p.dma("pool", W[i][:, 0:nk, 0:ncols], src, dW[i], writes=[rW[i]])
        if extra is not None:
            c1, off = extra
            src2 = wview[r0:r0 + nk * 128, c1:c1 + ncols].rearrange("(kc p) n -> p kc n", p=128)
            p.dma("pool", W[i][:, 0:nk, off:off + ncols], src2, dW[i], writes=[rW[i]])
        return i

    def mm_chunk(slot, col, nk, src, src_res, tiles, evac):
        banks = [next_ps() for _ in tiles]
        fns = []
        for kc in range(nk):
            for t, (c0, n) in enumerate(tiles):
                fns.append(lambda e, kc=kc, t=t, c0=c0, n=n: e.matmul(
                    PS[banks[t]][:, 0:n], lhsT=W[slot][:, kc, col:col + 128], rhs=src(kc, c0, n),
                    start=(kc == 0), stop=(kc == nk - 1), skip_group_check=True))
        p.group(fns, reads=[rW[slot]] + src_res, writes=[rPS[b] for b in banks])
        for t, (c0, n) in enumerate(tiles):
            evac(t, banks[t], c0, n)

    Hsrc = lambda kc, c0, n: H[:, kc, HALO + c0:HALO + c0 + n]

    def mlp(i):
        p.barrier()
        rmsnorm(P_NMLP + 16 * i, False, sq_off=KC * NT * 2)
        A = carve(0, KC * NT, BF16).rearrange("p (k n) -> p k n", k=KC)
        rA = [Res() for _ in range(KC)]
        rl = carve(KC * NT * 2 + 40000, 2 * 344, F32)
        rRL = [Res(), Res()]
        rc = [0]
        w1v = w_1.ap()[i]
        w2v = w_2.ap()[i]
        for q in range(4):
            for s in range(8):
                slot = load_stripe(w1v, 0, KC, q * 2048 + s * 256, 256)
                for oc in range(2):
                    ch = s * 2 + oc

                    def evac(t, b, c0, n, ch=ch):
                        k = rc[0] % 2
                        rc[0] += 1
                        tmp = rl[:, k * 344:k * 344 + n]
                        p.op("act", lambda e: e.activation(out=tmp, in_=PS[b][:, 0:n], func=AF.Relu),
                             reads=[rPS[b]], writes=[rRL[k]])
                        p.op("dve", lambda e: e.tensor_tensor(out=A[:, ch, c0:c0 + n], in0=tmp, in1=tmp, op=ALU.mult),
                             reads=[rRL[k]], writes=[rA[ch]])

                    mm_chunk(slot, oc * 128, KC, Hsrc, [rH], TILES, evac)
            for s in range(8):
                slot = load_stripe(w2v, q * 2048, KC, s * 256, 256)
                for oc in range(2):
                    o = s * 2 + oc

                    def evac(t, b, c0, n, o=o):
                        p.op("dve", lambda e: e.tensor_tensor(out=X[:, o, c0:c0 + n], in0=PS[b][:, 0:n],
                                                              in1=X[:, o, c0:c0 + n], op=ALU.add),
                             reads=[rPS[b], rX[o]], writes=[rX[o]])

                    mm_chunk(slot, oc * 128, KC, lambda kc, c0, n: A[:, kc, c0:c0 + n], rA, TILES, evac)

    def conv_layer(i, j):
        p.barrier()
        if i > 0:
            rXI, rXO = Res(), Res()
            dXI, dXC, dXL = p.dsem(), p.dsem(), p.dsem()
            p.dma("sp", xh_i.ap().rearrange("p (k n) -> p k n", k=KC), X[:, :, NP_ - HALO:NP_], dXI, reads=rX, writes=[rXI])
            p.coll([xh_i.ap().opt()], [xh_o.ap().opt()], dXC, reads=[rXI], writes=[rXO])
            p.dma("sp", Xh[:], xh_o.ap()[0:128, :].rearrange("p (k n) -> p k n", k=KC), dXL, reads=[rXO], writes=[rXh])
        pc = P_CONV + j * PC_SIZE
        C_OFF, G_OFF, SG_OFF = 0, KC * NT * 4, KC * NT * 4 + 4368
        rmsnorm(P_NMIX + 16 * i, True, sq_off=0)
        p.barrier()
        Cc = carve(C_OFF, KC * NT, F32).rearrange("p (k n) -> p k n", k=KC)
        G = carve(G_OFF, 1092, F32)
        SG = carve(SG_OFF, 2 * 354, F32)
        rG, rSG, rCc = Res(), [Res(), Res()], [Res() for _ in range(KC)]
        rCSI = Res()
        dCSI = p.dsem()
        p.dma("sp", CSI[:], cs_in.ap()[j].rearrange("(kc p) n -> p kc n", p=128), dCSI, writes=[rCSI])
        dO = p.dsem()
        out_sems.append(dO)
        csp_v = cs_p.ap()[j].rearrange("(kc p) n -> p kc n", p=128)
        css_v = cs_s.ap()[j].rearrange("(kc p) n -> p kc n", p=128)
        wv = w_pw1.ap()[j]
        sgc = [0]
        for kc in range(KC):
            if kc == 8:
                p.barrier()
            slot = load_stripe(wv, 0, KC, D + kc * 128, 128, extra=(kc * 128, 128))
            sig_t = {}

            def evac_g(t, b, c0, n, kc=kc):
                k = sgc[0] % 2
                sgc[0] += 1
                sig_t[t] = k
                p.op("act", lambda e: e.activation(out=SG[:, k * 354:k * 354 + n], in_=PS[b][:, 0:n], func=AF.Sigmoid,
                                                   bias=PARc(pc + PC_BG + kc), scale=1.0),
                     reads=[rPS[b], rPAR], writes=[rSG[k]])

            def evac_a(t, b, c0, n, kc=kc):
                k = sig_t[t]
                segs = [(c0, min(n, 1054 - c0), c0)] if c0 + n <= 1054 else [(c0, 1054 - c0, c0), (1054, 8, 1084)]
                for (h0, nn, g0) in segs:
                    o = h0 - c0
                    p.op("dve", lambda e, o=o, nn=nn, g0=g0: e.scalar_tensor_tensor(
                        out=G[:, g0:g0 + nn], in0=PS[b][:, o:o + nn], scalar=PARc(pc + PC_BA + kc),
                        in1=SG[:, k * 354 + o:k * 354 + o + nn], op0=ALU.add, op1=ALU.mult),
                         reads=[rPS[b], rSG[k], rPAR], writes=[rG])

            for t, (c0, n) in enumerate(CTILES):
                bg, ba = next_ps(), next_ps()
                fns = []
                for kk in range(KC):
                    fns.append(lambda e, kk=kk, c0=c0, n=n, bg=bg, slot=slot: e.matmul(PS[bg][:, 0:n], lhsT=W[slot][:, kk, 0:128],
                                                                         rhs=H[:, kk, c0:c0 + n], start=(kk == 0), stop=(kk == KC - 1), skip_group_check=True))
                for kk in range(KC):
                    fns.append(lambda e, kk=kk, c0=c0, n=n, ba=ba, slot=slot: e.matmul(PS[ba][:, 0:n], lhsT=W[slot][:, kk, 128:256],
                                                                         rhs=H[:, kk, c0:c0 + n], start=(kk == 0), stop=(kk == KC - 1), skip_group_check=True))
                p.group(fns, reads=[rW[slot], rH], writes=[rPS[bg], rPS[ba]])
                evac_g(t, bg, c0, n)
                evac_a(t, ba, c0, n)
            p.op("dve", lambda e: e.tensor_scalar(out=G[:, 0:HALO], in0=G[:, 0:HALO], scalar1=flag[:], scalar2=None, op0=ALU.mult),
                 reads=[rG, rC], writes=[rG])
            p.op("act", lambda e, kc=kc: e.copy(out=G[:, 1054:1084], in_=CSI[:, kc, :]), reads=[rCSI], writes=[rG])
            p.dma("sp", csp_v[:, kc, :], G[:, 1024:1054], dO, reads=[rG])
            p.dma("sp", css_v[:, kc, :], G[:, 1062:1092], dO, reads=[rG])
            wd = pc + PC_WDW + kc * 31
            for (o0, nn, g0) in ((0, NP_, 0), (NP_, NS, 1054)):
                p.op("dve", lambda e, o0=o0, nn=nn, g0=g0, kc=kc, wd=wd: e.tensor_scalar(
                    out=Cc[:, kc, o0:o0 + nn], in0=G[:, g0:g0 + nn], scalar1=PARc(wd), scalar2=PARc(pc + PC_BDW + kc),
                    op0=ALU.mult, op1=ALU.add), reads=[rG, rPAR], writes=[rCc[kc]])
                for tap in range(1, 31):
                    p.op("dve", lambda e, o0=o0, nn=nn, g0=g0, kc=kc, tap=tap, wd=wd: e.scalar_tensor_tensor(
                        out=Cc[:, kc, o0:o0 + nn], in0=G[:, g0 + tap:g0 + tap + nn], scalar=PARc(wd + tap),
                        in1=Cc[:, kc, o0:o0 + nn], op0=ALU.mult, op1=ALU.add), reads=[rG, rCc[kc]], writes=[rCc[kc]])
        p.barrier()
        T_OFF = G_OFF
        cb = carve(T_OFF, NT, BF16)
        cq = carve(T_OFF + NT * 2, NT, BF16)
        rcb, rcq = Res(), Res()
        bs = [next_ps() for _ in range(6)]
        for kc in range(KC):
            p.op("act", lambda e, kc=kc: e.activation(out=cb, in_=Cc[:, kc, :], func=AF.Copy), reads=[rCc[kc]], writes=[rcb])
            p.op("act", lambda e, kc=kc: e.activation(out=cq, in_=Cc[:, kc, :], func=AF.Square), reads=[rCc[kc]], writes=[rcq])
            fns = []
            for t, (c0, n) in enumerate(TILES):
                fns.append(lambda e, t=t, c0=c0, n=n, kc=kc: e.matmul(PS[bs[t]][:, 0:n], lhsT=ones[:], rhs=cb[:, c0:c0 + n],
                                                                     start=(kc == 0), stop=(kc == KC - 1), skip_group_check=True))
                fns.append(lambda e, t=t, c0=c0, n=n, kc=kc: e.matmul(PS[bs[3 + t]][:, 0:n], lhsT=ones[:], rhs=cq[:, c0:c0 + n],
                                                                     start=(kc == 0), stop=(kc == KC - 1), skip_group_check=True))
            p.group(fns, reads=[rcb, rcq, rC], writes=[rPS[b] for b in bs])
        p.barrier()
        mu = carve(T_OFF, NT, F32)
        R_OFF = SG_OFF + 2832
        rstd = carve(R_OFF, NT, F32)
        rmu, rrs = Res(), Res()
        for t, (c0, n) in enumerate(TILES):
            p.op("act", lambda e, t=t, c0=c0, n=n: e.activation(out=mu[:, c0:c0 + n], in_=PS[bs[t]][:, 0:n], func=AF.Identity, scale=1.0 / D),
                 reads=[rPS[bs[t]]], writes=[rmu])
            p.op("dve", lambda e, c0=c0, n=n: e.tensor_tensor(out=rstd[:, c0:c0 + n], in0=mu[:, c0:c0 + n], in1=mu[:, c0:c0 + n], op=ALU.mult),
                 reads=[rmu], writes=[rrs])
            p.op("dve", lambda e, t=t, c0=c0, n=n: e.scalar_tensor_tensor(out=rstd[:, c0:c0 + n], in0=PS[bs[3 + t]][:, 0:n], scalar=1.0 / D,
                                                                       in1=rstd[:, c0:c0 + n], op0=ALU.mult, op1=ALU.subtract),
                 reads=[rPS[bs[3 + t]], rrs], writes=[rrs])
            p.op("act", lambda e, c0=c0, n=n: e.activation(out=rstd[:, c0:c0 + n], in_=rstd[:, c0:c0 + n], func=AF.Sqrt, bias=epsl[:], scale=1.0),
                 reads=[rrs, rC], writes=[rrs])
            p.op("dve", lambda e, c0=c0, n=n: e.reciprocal(out=rstd[:, c0:c0 + n], in_=rstd[:, c0:c0 + n]), reads=[rrs], writes=[rrs])
        for kc in range(KC):
            p.op("dve", lambda e, kc=kc: e.tensor_tensor(out=Cc[:, kc, :], in0=Cc[:, kc, :], in1=mu[:, :], op=ALU.subtract),
                 reads=[rCc[kc], rmu], writes=[rCc[kc]])
            p.op("pool", lambda e, kc=kc: e.tensor_tensor(out=Cc[:, kc, :], in0=Cc[:, kc, :], in1=rstd[:, :], op=ALU.mult),
                 reads=[rCc[kc], rrs], writes=[rCc[kc]])
            p.op("act", lambda e, kc=kc: e.activation(out=H[:, kc, HALO:NH], in_=Cc[:, kc, :], func=AF.Silu,
                                                      bias=PARc(pc + PC_LNB + kc), scale=PARc(pc + PC_LNG + kc)),
                 reads=[rCc[kc], rPAR], writes=[rH])
        w2v = w_pw2.ap()[j]
        for s in range(8):
            slot = load_stripe(w2v, 0, KC, s * 256, 256)
            for oc in range(2):
                o = s * 2 + oc

                def evac(t, b, c0, n, o=o):
                    p.op("dve", lambda e: e.scalar_tensor_tensor(out=X[:, o, c0:c0 + n], in0=PS[b][:, 0:n],
                                                                 scalar=PARc(pc + PC_BPW2 + o), in1=X[:, o, c0:c0 + n],
                                                                 op0=ALU.add, op1=ALU.add),
                         reads=[rPS[b], rX[o], rPAR], writes=[rX[o]])

                mm_chunk(slot, oc * 128, KC, Hsrc, [rH], TILES, evac)


    def mm(out, lhsT, rhs, start, stop):
        return lambda e: e.matmul(out, lhsT=lhsT, rhs=rhs, start=start, stop=stop, skip_group_check=True)

    def attn_layer(i, j):
        p.barrier()
        A0 = 3328
        OT = carve(A0, 8 * NT, BF16).rearrange("p (h n) -> p h n", h=8)
        B0 = A0 + 8 * NT * 2
        QT = carve(B0, 6 * NT, BF16).rearrange("p (a g n) -> p a g n", a=2, g=3)
        C0 = B0 + 6 * NT * 2
        rmsnorm(P_NMIX + 16 * i, False, sq_off=C0)
        p.barrier()
        mgen, mfirst, mg2 = carve(0, 512, BF16), carve(1024, 512, BF16), carve(2048, 512, BF16)
        msc, msn = carve(3072, 104, BF16), carve(3280, 24, BF16)
        rM = Res()
        dM = p.dsem()
        for dst, src in ((mgen, mgen_d), (mfirst, mfirst_d), (mg2, mg2_d), (msc, msc_d)):
            p.dma("pool", dst, src.ap(), dM, writes=[rM])
        p.dma("pool", msn, msn_d.ap(), dM, writes=[rM])
        wq = w_qkv.ap()[j]
        if stop == "A0":
            return
        KF = [carve(C0 + k * 4128, NT, F32) for k in range(2)]
        KB = [carve(C0 + 8256 + k * 2064, NT, BF16) for k in range(2)]
        VF = [carve(C0 + 12384 + k * 1024, 256, F32) for k in range(2)]
        VB = [carve(C0 + 14432 + k * 512, 256, BF16) for k in range(2)]
        rKF, rKB, rVF, rVB = [Res(), Res()], [Res(), Res()], [Res(), Res()], [Res(), Res()]
        dKF, dKB, dVF, dVB = [p.dsem(), p.dsem()], [p.dsem(), p.dsem()], [p.dsem(), p.dsem()], [p.dsem(), p.dsem()]
        out_sems.extend(dKF + dVF)
        rEK, rEV = Res(), Res()
        rSK, rSV = [Res() for _ in range(3)], [Res() for _ in range(3)]
        exk_v = exk[j].ap().rearrange("p (g h t) -> p g h t", g=3, h=8)
        exv_v = exv[j].ap().rearrange("(g h t) e -> g h t e", g=3, h=8)
        cnt = [0, 0]
        for g in range(3):
            for hp in range(4):
                slot = load_stripe(wq, 0, KC, g * 3072 + 1024 + hp * 256, 256)
                for oc in range(2):
                    h = hp * 2 + oc
                    k = cnt[0] % 2
                    cnt[0] += 1

                    def evac(t, b, c0, n, k=k):
                        p.op("act", lambda e: e.activation(out=KF[k][:, c0:c0 + n], in_=PS[b][:, 0:n], func=AF.Copy),
                             reads=[rPS[b]], writes=[rKF[k]])
                        p.op("dve", lambda e: e.tensor_copy(out=KB[k][:, c0:c0 + n], in_=KF[k][:, c0:c0 + n]),
                             reads=[rKF[k]], writes=[rKB[k]])

                    mm_chunk(slot, oc * 128, KC, Hsrc, [rH], TILES, evac)
                    p.dma("sp", kT_out.ap()[j, g, h], KF[k][:, :], dKF[k], reads=[rKF[k]])
                    p.dma("sp", exk_v[:, g, h, :], KB[k][:, :], dKB[k], reads=[rKB[k]], writes=[rEK])
                    nd = NEED[g]
                    p.dma("sp", sk[j][g].ap()[:, h * nd:(h + 1) * nd], KB[k][:, NP_ - nd:NP_], dKB[k], reads=[rKB[k]], writes=[rSK[g]])
                if stop == "AK":
                    continue
                slot = load_stripe(wq, 0, KC, g * 3072 + 2048 + hp * 256, 256)
                for tb in range(9 if stop != "AV8" else 8):
                    M = 128 if tb < 8 else NS
                    t0 = tb * 128
                    b = next_ps()
                    p.group([mm(PS[b][0:M, 0:256], H[:, kc, HALO + t0:HALO + t0 + M], W[slot][:, kc, 0:256], kc == 0, kc == KC - 1)
                             for kc in range(KC)], reads=[rW[slot], rH], writes=[rPS[b]])
                    k = cnt[1] % 2
                    cnt[1] += 1
                    p.op("act", lambda e, k=k, b=b, M=M: e.activation(out=VF[k][0:M, :], in_=PS[b][0:M, 0:256], func=AF.Copy),
                         reads=[rPS[b]], writes=[rVF[k]])
                    p.op("dve", lambda e, k=k, M=M: e.tensor_copy(out=VB[k][0:M, :], in_=VF[k][0:M, :]),
                         reads=[rVF[k]], writes=[rVB[k]])
                    p.dma("sp", v_out.ap()[j, g, t0:t0 + M, hp * 256:hp * 256 + 256], VF[k][0:M, :], dVF[k], reads=[rVF[k]])
                    dst = exv_v[g, hp * 2:hp * 2 + 2, t0:t0 + M, :].rearrange("h t e -> t h e")
                    p.dma("sp", dst, VB[k][0:M, :].rearrange("t (h e) -> t h e", h=2), dVB[k], reads=[rVB[k]], writes=[rEV])
                    nd = NEED[g]
                    if tb < 8 and t0 >= NP_ - nd:
                        tl_ = t0 - (NP_ - nd)
                        sva = sv[j][g].ap()
                        dst2 = bass.AP(sva.tensor, sva.offset + ((hp * 2) * nd + tl_) * 128, [[128, M], [nd * 128, 2], [1, 128]])
                        p.dma("sp", dst2, VB[k][0:M, :].rearrange("t (h e) -> t h e", h=2), dVB[k], reads=[rVB[k]], writes=[rSV[g]])
        if stop in ("A", "AK", "AV8"):
            return
        rRK, rRV = [Res() for _ in range(3)], [Res() for _ in range(3)]
        for g in range(3):
            p.coll([sk[j][g].ap().opt()], [rk[j][g].ap().opt()], p.dsem(), reads=[rSK[g]], writes=[rRK[g]])
            p.coll([sv[j][g].ap().opt()], [rv[j][g].ap().opt()], p.dsem(), reads=[rSV[g]], writes=[rRV[g]])
        p.barrier()
        if stop == "B":
            return
        o = [C0]

        def take(cols, dt):
            ap = carve(o[0], cols, dt)
            o[0] += cols * (4 if dt == F32 else 2)
            return ap

        ownK = take(3 * NT, BF16).rearrange("p (g n) -> p g n", g=3)
        pK0, pK1, pK2 = take(128, BF16), take(512, BF16), take(1024, BF16)
        Vg0 = take(8 * 128, BF16).rearrange("p (b e) -> p b e", b=8)
        Vp0 = take(128, BF16)
        Vg1 = take(8 * 128, BF16).rearrange("p (b r e) -> p b r e", b=2, r=4)
        Vp1 = take(4 * 128, BF16).rearrange("p (r e) -> p r e", r=4)
        Vg2 = take(16 * 128, BF16).rearrange("p (r e) -> p r e", r=16)
        Vs = take(3 * 128, BF16).rearrange("p (g e) -> p g e", g=3)
        cK = take(13 * 128, BF16)
        cV = take(13 * 128, BF16).rearrange("p (c e) -> p c e", c=13)
        accO, accD = take(NT, F32), take(NT, F32)
        Pt = [take(512, BF16) for _ in range(2)]
        Pc, Pn = take(104, BF16), take(24, BF16)
        assert o[0] <= ARENA_B, o[0]
        rHDl = [Res() for _ in range(13)]
        dHD = [p.dsem() for _ in range(4)]
        rQT = [Res(), Res()]
        rOT = Res()
        racc = Res()
        rPt = [Res(), Res()]
        rPc = Res()
        SB_, OB_, DB_ = (0, 1), (2, 3), (4, 5)
        uc = [0, 0]
        ckv = ckT.ap()[j]
        cvv = cv.ap()[j]
        ev = exv[j].ap()

        def rows(base_ap, r0, pstep, np_, dims):
            return bass.AP(base_ap.tensor, base_ap.offset + r0 * 128, [[pstep * 128, np_]] + [[st * 128, c] for (st, c) in dims] + [[1, 128]])

        for hp in range(4):
            for g in range(3):
                slot = load_stripe(wq, 0, KC, g * 3072 + hp * 256, 256)
                for oc in range(2):
                    def evac(t, b, c0, n, oc=oc, g=g):
                        p.op("act", lambda e: e.activation(out=QT[:, oc, g, c0:c0 + n], in_=PS[b][:, 0:n], func=AF.Copy),
                             reads=[rPS[b]], writes=[rQT[oc]])
                    mm_chunk(slot, oc * 128, KC, Hsrc, [rH], TILES, evac)
            if stop == "Q":
                return
            for hh in range(2):
                h = hp * 2 + hh
                if stop in ("L", "G0", "G1", "G2") and h > 0:
                    return
                qt = lambda g, sl, hh=hh: QT[:, hh, g, sl]
                p.dma("sp", ownK, exk_v[:, :, h, :], dHD[0], reads=[rEK], writes=[rHDl[0]])
                p.dma("sp", pK0, rk[j][0].ap()[0:128, h * 128:(h + 1) * 128], dHD[0], reads=[rRK[0]], writes=[rHDl[1]])
                p.dma("sp", pK1, rk[j][1].ap()[0:128, h * 512:(h + 1) * 512], dHD[0], reads=[rRK[1]], writes=[rHDl[2]])
                p.dma("sp", pK2, rk[j][2].ap()[0:128, h * 1024:(h + 1) * 1024], dHD[0], reads=[rRK[2]], writes=[rHDl[3]])
                r_g = lambda g, h=h: (g * 8 + h) * NT
                p.dma("sp", Vg0, rows(ev, r_g(0), 1, 128, [(128, 8)]), dHD[1], reads=[rEV], writes=[rHDl[4]])
                p.dma("sp", Vp0, rows(rv[j][0].ap(), h * 128, 1, 128, []), dHD[1], reads=[rRV[0]], writes=[rHDl[5]])
                for b_ in range(2):
                    p.dma("sp", Vg1[:, b_], rows(ev, r_g(1) + 512 * b_, 4, 128, [(1, 4)]), dHD[1], reads=[rEV], writes=[rHDl[6]])
                p.dma("sp", Vp1, rows(rv[j][1].ap(), h * 512, 4, 128, [(1, 4)]), dHD[1], reads=[rRV[1]], writes=[rHDl[7]])
                p.dma("sp", Vg2[64:128], rows(ev, r_g(2), 16, 64, [(1, 16)]), dHD[2], reads=[rEV], writes=[rHDl[8]])
                p.dma("sp", Vg2[0:64], rows(rv[j][2].ap(), h * 1024, 16, 64, [(1, 16)]), dHD[2], reads=[rRV[2]], writes=[rHDl[9]])
                p.dma("sp", Vs[0:8], rows(ev, r_g(0) + NP_, 1, 8, [(8 * NT, 3)]), dHD[2], reads=[rEV], writes=[rHDl[10]])
                p.dma("pool", cK, ckv[:, h, :], dHD[3], writes=[rHDl[11]])
                p.dma("pool", cV, cvv[:, :, h * 128:(h + 1) * 128], dHD[3], writes=[rHDl[12]])

                def s_tile(mask, smm):
                    sb_ = SB_[uc[0] % 2]
                    k = uc[0] % 2
                    uc[0] += 1
                    fns = [mm(PS[sb_][:, 0:512], ident[:], mask, True, False)]
                    lst = smm(PS[sb_])
                    for idx, (out_, l_, r_) in enumerate(lst):
                        fns.append(mm(out_, l_, r_, False, idx == len(lst) - 1))
                    p.group(fns, reads=rHDl + [rM, rC, rQT[hh]], writes=[rPS[sb_]])
                    p.op("act", lambda e: e.activation(out=Pt[k][:, :], in_=PS[sb_][:, 0:512], func=AF.Exp, scale=SCALE),
                         reads=[rPS[sb_]], writes=[rPt[k]])
                    return k

                def od_banks():
                    ob, db = OB_[uc[1] % 2], DB_[uc[1] % 2]
                    uc[1] += 1
                    return ob, db

                def accum(ob, db, view, first):
                    for bank, acc in ((ob, accO), (db, accD)):
                        dst = view(acc)
                        src = PS[bank][:, 0:512]
                        if len(dst.shape) == 3:
                            src = src.rearrange("p (a b) -> p a b", a=dst.shape[1])
                        elif len(dst.shape) == 4:
                            src = src.rearrange("p (a b c) -> p a b c", a=dst.shape[1], b=dst.shape[2])
                        if first:
                            p.op("dve", lambda e, dst=dst, src=src: e.tensor_copy(out=dst, in_=src), reads=[rPS[bank]], writes=[racc])
                        else:
                            p.op("dve", lambda e, dst=dst, src=src: e.tensor_tensor(out=dst, in0=src, in1=dst, op=ALU.add),
                                 reads=[rPS[bank], racc], writes=[racc])

                if stop == "L":
                    continue
                for n0 in (0, 4):
                    ob, db = od_banks()
                    pv, dn = [], []
                    for n1 in (n0, n0 + 2):
                        def smm(S, n1=n1):
                            r = []
                            for u in range(2):
                                n = n1 + u
                                q = qt(0, slice(n * 128, n * 128 + 128))
                                kp = pK0[:, :] if n == 0 else ownK[:, 0, (n - 1) * 128:n * 128]
                                r.append((S[:, u * 256:u * 256 + 128], kp, q))
                                r.append((S[:, u * 256 + 128:u * 256 + 256], ownK[:, 0, n * 128:n * 128 + 128], q))
                            return r
                        k = s_tile(mfirst if n1 == 0 else mgen, smm)
                        for u in range(2):
                            n = n1 + u
                            oc_ = (n - n0) * 128
                            vp = Vp0[:, :] if n == 0 else Vg0[:, n - 1, :]
                            pv.append((k, mm(PS[ob][:, oc_:oc_ + 128], vp, Pt[k][:, u * 256:u * 256 + 128], True, False)))
                            pv.append((k, mm(PS[ob][:, oc_:oc_ + 128], Vg0[:, n, :], Pt[k][:, u * 256 + 128:u * 256 + 256], False, True)))
                            dn.append((k, mm(PS[db][:, oc_:oc_ + 128], ones[:], Pt[k][:, u * 256:u * 256 + 128], True, False)))
                            dn.append((k, mm(PS[db][:, oc_:oc_ + 128], ones[:], Pt[k][:, u * 256 + 128:u * 256 + 256], False, True)))
                    p.group([f for _, f in pv], reads=[rPt[0], rPt[1]] + rHDl, writes=[rPS[ob]])
                    p.group([f for _, f in dn], reads=[rPt[0], rPt[1], rC], writes=[rPS[db]])
                    accum(ob, db, lambda acc, n0=n0: acc[:, n0 * 128:n0 * 128 + 512], True)
                if stop == "G0":
                    continue
                for r0 in (0, 2):
                    ob, db = od_banks()
                    pv, dn = [], []
                    for r in (r0, r0 + 1):
                        def smm(S, r=r):
                            res_ = []
                            for b_ in range(2):
                                q = qt(1, slice(512 * b_ + r, 512 * b_ + 512, 4))
                                kp = pK1[:, r:512:4] if b_ == 0 else ownK[:, 1, r:512:4]
                                res_.append((S[:, b_ * 256:b_ * 256 + 128], kp, q))
                                res_.append((S[:, b_ * 256 + 128:b_ * 256 + 256], ownK[:, 1, 512 * b_ + r:512 * b_ + 512:4], q))
                            return res_
                        k = s_tile(mfirst, smm)
                        for b_ in range(2):
                            oc_ = ((r - r0) * 2 + b_) * 128
                            vp = Vp1[:, r, :] if b_ == 0 else Vg1[:, 0, r, :]
                            pv.append(mm(PS[ob][:, oc_:oc_ + 128], vp, Pt[k][:, b_ * 256:b_ * 256 + 128], True, False))
                            pv.append(mm(PS[ob][:, oc_:oc_ + 128], Vg1[:, b_, r, :], Pt[k][:, b_ * 256 + 128:b_ * 256 + 256], False, True))
                            dn.append(mm(PS[db][:, oc_:oc_ + 128], ones[:], Pt[k][:, b_ * 256:b_ * 256 + 128], True, False))
                            dn.append(mm(PS[db][:, oc_:oc_ + 128], ones[:], Pt[k][:, b_ * 256 + 128:b_ * 256 + 256], False, True))
                    p.group(pv, reads=[rPt[0], rPt[1]] + rHDl, writes=[rPS[ob]])
                    p.group(dn, reads=[rPt[0], rPt[1], rC], writes=[rPS[db]])
                    accum(ob, db, lambda acc, r0=r0: acc[:, 0:NP_].rearrange("p (b i f) -> p f b i", b=2, f=4)[:, r0:r0 + 2, :, :], False)
                if stop == "G1":
                    continue
                for half in range(2):
                    ob, db = od_banks()

                    def smm(S, half=half):
                        res_ = []
                        for c in range(8):
                            r = 8 * half + c
                            q = qt(2, slice(r, NP_, 16))
                            res_.append((S[0:64, c * 64:c * 64 + 64], pK2[:, r:NP_:16], q))
                            res_.append((S[64:128, c * 64:c * 64 + 64], ownK[:, 2, r:NP_:16], q))
                        return res_
                    k = s_tile(mg2, smm)
                    p.group([mm(PS[ob][:, c * 64:c * 64 + 64], Vg2[:, 8 * half + c, :], Pt[k][:, c * 64:c * 64 + 64], True, True) for c in range(8)],
                            reads=[rPt[k]] + rHDl, writes=[rPS[ob]])
                    p.group([mm(PS[db][:, 0:512], ones[:], Pt[k][:, :], True, True)], reads=[rPt[k], rC], writes=[rPS[db]])
                    accum(ob, db, lambda acc, half=half: acc[:, 0:NP_].rearrange("p (i r) -> p r i", r=16)[:, 8 * half:8 * half + 8, :], False)
                if stop == "G2":
                    continue
                grp = [0] + [1] * 4 + [2] * 8
                fns = [mm(PS[6][:, 0:104], ident[:], msc, True, False), mm(PS[6][0:8, 104:128], ident[0:8, 0:8], msn[0:8, :], False, False)]
                for c in range(13):
                    fns.append(mm(PS[6][:, c * 8:c * 8 + 8], cK[:, c * 128:c * 128 + 128], qt(grp[c], slice(NP_, NT)), False, False))
                for g in range(3):
                    fns.append(mm(PS[6][0:8, 104 + g * 8:112 + g * 8], ownK[:, g, NP_:NT], qt(g, slice(NP_, NT)), False, g == 2))
                p.group(fns, reads=rHDl + [rM, rC, rQT[hh]], writes=[rPS[6]])
                p.op("act", lambda e: e.activation(out=Pc, in_=PS[6][:, 0:104], func=AF.Exp, scale=SCALE), reads=[rPS[6]], writes=[rPc])
                p.op("act", lambda e: e.activation(out=Pn[0:8, :], in_=PS[6][0:8, 104:128], func=AF.Exp, scale=SCALE), reads=[rPS[6]], writes=[rPc])
                fo, fd = [], []
                for c in range(13):
                    fo.append(mm(PS[7][:, 0:8], cV[:, c, :], Pc[:, c * 8:c * 8 + 8], c == 0, False))
                    fd.append(mm(PS[7][:, 8:16], ones[:], Pc[:, c * 8:c * 8 + 8], c == 0, False))
                for g in range(3):
                    fo.append(mm(PS[7][:, 0:8], Vs[0:8, g, :], Pn[0:8, g * 8:g * 8 + 8], False, g == 2))
                    fd.append(mm(PS[7][:, 8:16], ones[0:8, :], Pn[0:8, g * 8:g * 8 + 8], False, g == 2))
                p.group(fo + fd, reads=[rPc, rC] + rHDl, writes=[rPS[7]])
                p.op("dve", lambda e: e.tensor_copy(out=accO[:, NP_:NT], in_=PS[7][:, 0:8]), reads=[rPS[7]], writes=[racc])
                p.op("dve", lambda e: e.tensor_copy(out=accD[:, NP_:NT], in_=PS[7][:, 8:16]), reads=[rPS[7]], writes=[racc])
                p.op("dve", lambda e: e.reciprocal(out=accD, in_=accD), reads=[racc], writes=[racc])
                p.op("dve", lambda e, h=h: e.tensor_tensor(out=OT[:, h, :], in0=accO, in1=accD, op=ALU.mult), reads=[racc], writes=[rOT])
        wov = w_o.ap()[j]
        for s_ in range(8):
            slot = load_stripe(wov, 0, 8, s_ * 256, 256)
            for oc in range(2):
                oo = s_ * 2 + oc

                def evac(t, b, c0, n, oo=oo):
                    p.op("dve", lambda e: e.tensor_tensor(out=X[:, oo, c0:c0 + n], in0=PS[b][:, 0:n], in1=X[:, oo, c0:c0 + n], op=ALU.add),
                         reads=[rPS[b], rX[oo]], writes=[rX[oo]])

                mm_chunk(slot, oc * 128, 8, lambda kc, c0, n: OT[:, kc, c0:c0 + n], [rOT], TILES, evac)

    for i in range(4):
        j = i // 2
        if stop is not None and i >= 2:
            break
        if i % 2 == 0:
            if stage >= 2:
                conv_layer(i, j)
        else:
            if stage >= 3:
                attn_layer(i, j)
        if stage >= 1 and not (stop is not None and i == 1):
            mlp(i)
    p.barrier()
    YF = carve(0, KC * NT, F32).rearrange("p (k n) -> p k n", k=KC)
    rstd_f = carve(KC * NT * 4, NT, F32)
    rY = [Res() for _ in range(KC)]
    dY = p.dsem()
    out_sems.append(dY)
    yv = yT.ap().rearrange("(kc p) n -> p kc n", p=128)
    for (c0, n) in TILES:
        sq = H[:, :, 0:n]
        rq, rr = Res(), Res()
        p.op("act", lambda e, c0=c0, n=n, sq=sq: e.activation(out=sq, in_=X[:, :, c0:c0 + n], func=AF.Square), reads=rX, writes=[rq])
        b = next_ps(6, 8)
        p.group([lambda e, kc=kc, b=b, n=n, sq=sq: e.matmul(PS[b][:, 0:n], lhsT=ones[:], rhs=sq[:, kc, :],
                                                           start=(kc == 0), stop=(kc == KC - 1), skip_group_check=True) for kc in range(KC)],
                reads=[rq, rC], writes=[rPS[b]])
        p.op("act", lambda e, b=b, n=n, c0=c0: e.activation(out=rstd_f[:, c0:c0 + n], in_=PS[b][:, 0:n], func=AF.Sqrt, bias=epsr[:], scale=1.0 / D),
             reads=[rPS[b], rC], writes=[rr])
        p.op("dve", lambda e, n=n, c0=c0: e.reciprocal(out=rstd_f[:, c0:c0 + n], in_=rstd_f[:, c0:c0 + n]), reads=[rr], writes=[rr])
        for kc in range(KC):
            p.op("dve", lambda e, kc=kc, c0=c0, n=n: e.scalar_tensor_tensor(
                out=YF[:, kc, c0:c0 + n], in0=X[:, kc, c0:c0 + n], scalar=PARc(P_NFIN + kc), in1=rstd_f[:, c0:c0 + n],
                op0=ALU.mult, op1=ALU.mult), reads=[rr, rPAR, rX[kc]], writes=[rY[kc]])
    for q in range(4):
        p.dma("sp", yv[:, 4 * q:4 * q + 4, :], YF[:, 4 * q:4 * q + 4, :], dY, reads=rY[4 * q:4 * q + 4])
    p.final_wait(out_sems)
    with nc.Block() as block:
        p.emit(block)
    es.close()
    return nc


def _fm(v):
    return np.ascontiguousarray(np.moveaxis(v.reshape(v.shape[:-1] + (KC, 128)), -1, 0))


def _pack_params(inp):
    par = np.zeros((128, NPAR), np.float32)
    par[:, P_NMIX:P_NMIX + 64] = _fm(inp["norm_mix"]).reshape(128, 64)
    par[:, P_NMLP:P_NMLP + 64] = _fm(inp["norm_mlp"]).reshape(128, 64)
    par[:, P_NFIN:P_NFIN + 16] = _fm(inp["norm_final"])
    for j in range(2):
        pc = P_CONV + j * PC_SIZE
        b1 = inp["conv_b_pw1"][j]
        par[:, pc + PC_BA:pc + PC_BA + 16] = _fm(b1[:D])
        par[:, pc + PC_BG:pc + PC_BG + 16] = _fm(b1[D:])
        wd = _fm(inp["conv_w_dw"][j])
        par[:, pc + PC_WDW:pc + PC_WDW + 496] = np.transpose(wd, (0, 2, 1)).reshape(128, 496)
        par[:, pc + PC_BDW:pc + PC_BDW + 16] = _fm(inp["conv_b_dw"][j])
        par[:, pc + PC_LNG:pc + PC_LNG + 16] = _fm(inp["conv_ln_g"][j])
        par[:, pc + PC_LNB:pc + PC_LNB + 16] = _fm(inp["conv_ln_b"][j])
        par[:, pc + PC_BPW2:pc + PC_BPW2 + 16] = _fm(inp["conv_b_pw2"][j])
    return par


def _masks(role):
    k = np.arange(128)[:, None]
    q = np.arange(128)[None, :]
    prev_ok = (q <= k)
    own_ok = (k <= q)
    gen = np.concatenate([prev_ok, own_ok], 1)
    first = np.concatenate([prev_ok if role == 1 else np.zeros_like(prev_ok), own_ok], 1)
    m = lambda ok: np.where(ok, 0.0, NEG).astype(np.float32)
    mgen2 = m(np.concatenate([gen, gen], 1))
    mfirst2 = m(np.concatenate([first, gen], 1))
    kk = np.arange(128)[:, None]
    qq = np.arange(64)[None, :]
    g2 = np.where(kk < 64, (role == 1) & (qq >= 0), (kk - 64) <= qq)
    mg2x8 = m(np.tile(g2, (1, 8)))
    i = np.arange(128)[:, None]
    s = np.arange(8)[None, :]
    cols = [i >= s]
    for r in range(4):
        cols.append(((s % 4) == r) & (i >= s // 4))
    for r in range(8):
        cols.append((s == r) & (i >= 0))
    ms_cache = m(np.concatenate(cols, 1))
    jn = np.arange(8)[:, None]
    ms_new = m(np.concatenate([jn <= s, (jn <= s) & ((s - jn) % 4 == 0), jn == s], 1))
    ms_new = np.concatenate([ms_new, np.zeros((120, 24), np.float32)], 0)
    return mgen2, mfirst2, mg2x8, ms_cache, ms_new


def _prep_inputs(inp):
    x_p, x_s = inp["x_prompt"], inp["x_sample"]
    par = _pack_params(inp)
    caches = [inp["cache_kv_w128"], inp["cache_kv_w512"], inp["cache_kv_w2048"]]
    dil = [1, 4, 16]
    maps = []
    for c in range(8):
        b, role = c // 2, c % 2
        t0 = role * NP_
        xT = np.concatenate([x_p[b, t0:t0 + NP_].T, x_s[c].T], 1)
        xh0 = x_p[b, t0 - HALO:t0].T if role == 1 else np.zeros((D, HALO), np.float32)
        cs_in = np.transpose(inp["state_conv"][:, c], (0, 2, 1))
        ckT = np.zeros((2, 128, 8, 13 * 128), np.float32)
        cv = np.zeros((2, 128, 13, 1024), np.float32)
        cls = 0
        for g in range(3):
            d = dil[g]
            ncls = [1, 4, 8][g]
            kv = caches[g][:, c]
            for r in range(ncls):
                rows = kv[:, r::d][:, :128]
                ckT[:, :, :, cls * 128:(cls + 1) * 128] = np.transpose(rows[:, :, 0], (0, 3, 2, 1))
                cv[:, :, cls, :] = rows[:, :, 1].reshape(2, 128, 1024)
                cls += 1
        mgen2, mfirst2, mg2x8, ms_cache, ms_new = _masks(role)
        m = {
            "xT": np.ascontiguousarray(xT), "xh0": np.ascontiguousarray(xh0), "cs_in": np.ascontiguousarray(cs_in),
            "par": par, "flag": np.full((128, 1), float(role), np.float32),
            "mgen2": mgen2, "mfirst2": mfirst2, "mg2x8": mg2x8, "ms_cache": ms_cache, "ms_new": ms_new,
            "ckT": ckT, "cv": cv,
        }
        for k in ("conv_w_pw1", "conv_w_pw2", "attn_w_qkv", "attn_w_o", "mlp_w1", "mlp_w2"):
            m[k] = inp[k]
        maps.append(m)
    return maps


def _assemble(res):
    y_p = np.zeros((4, 2048, D), np.float32)
    y_s = np.zeros((8, NS, D), np.float32)
    csp = np.zeros((2, 4, HALO, D), np.float32)
    css = np.zeros((2, 8, HALO, D), np.float32)
    wins = [128, 512, 2048]
    kvp = [np.zeros((2, 4, w, 2, 8, 128), np.float32) for w in wins]
    kvs = [np.zeros((2, 8, NS, 2, 8, 128), np.float32) for _ in wins]
    for c in range(8):
        r = res[c]
        b, role = c // 2, c % 2
        t0 = role * NP_
        yT = r["yT"]
        y_p[b, t0:t0 + NP_] = yT[:, :NP_].T
        y_s[c] = yT[:, NP_:].T
        css[:, c] = np.transpose(r["cs_s"], (0, 2, 1))
        if role == 1:
            csp[:, b] = np.transpose(r["cs_p"], (0, 2, 1))
        kT = r["kT_out"]
        vo = r["v_out"]
        for g in range(3):
            kk = np.transpose(kT[:, g], (0, 3, 1, 2))
            vv = vo[:, g].reshape(2, NT, 8, 128)
            kvs[g][:, c, :, 0] = kk[:, NP_:]
            kvs[g][:, c, :, 1] = vv[:, NP_:]
            w = wins[g]
            lo = max(2048 - w, t0)
            hi = t0 + NP_
            if lo < hi:
                dst = slice(lo - (2048 - w), hi - (2048 - w))
                kvp[g][:, b, dst, 0] = kk[:, lo - t0:hi - t0]
                kvp[g][:, b, dst, 1] = vv[:, lo - t0:hi - t0]
    return (y_p, y_s, csp, css, kvp[0], kvs[0], kvp[1], kvs[1], kvp[2], kvs[2])


_NC_CACHE = {}


def kernel(**inputs):
    inp = {k: np.asarray(v) for k, v in inputs.items()}
    if "nc" not in _NC_CACHE:
        _NC_CACHE["nc"] = build()
    maps = _prep_inputs(inp)
    res = run_bass_kernel_spmd(_NC_CACHE["nc"], maps, core_ids=list(range(8)))
    return _assemble(res.results)
```

```python
import numpy as np
import concourse.bass as bass
import concourse.mybir as mybir
from concourse.bass_utils import run_bass_kernel_spmd

F32 = mybir.dt.float32
BF16 = mybir.dt.bfloat16
AF = mybir.ActivationFunctionType
ALU = mybir.AluOpType

D = 2048
KC = 16
NP_ = 1024
NS = 8
NT = 1032
HALO = 30
NH = NT + HALO
DFF = 8192
RMS_EPS = 1e-6
LN_EPS = 1e-5
SCALE = 128 ** -0.5
NEG = -30000.0
TILES = [(0, 344), (344, 344), (688, 344)]
CTILES = [(0, 354), (354, 354), (708, 354)]

P_NMIX, P_NMLP, P_NFIN = 0, 64, 128
P_CONV = 144
PC_BA, PC_BG, PC_WDW, PC_BDW, PC_LNG, PC_LNB, PC_BPW2, PC_SIZE = 0, 16, 32, 528, 544, 560, 576, 592
NPAR = P_CONV + 2 * PC_SIZE

XK = 0
XV = 3 * 8 * NT
KVX = XV


class Res:
    __slots__ = ("w", "r")

    def __init__(self):
        self.w = None
        self.r = {}


class DmaSem:
    def __init__(self, sem):
        self.sem = sem
        self.n = 0
        self.unit = 16


class Eng:
    def __init__(self, name, sem):
        self.name = name
        self.sem = sem
        self.n = 0
        self.ins = []
        self.known = {}


class Prog:
    def __init__(self, nc, es):
        self.nc = nc
        self.es = es
        self.E = {}
        for nm in ("pe", "act", "dve", "pool", "sp"):
            self.E[nm] = Eng(nm, es.enter_context(nc.semaphore("s_" + nm)))
        self.nsem = 0
        self.dsems = []

    def dsem(self):
        self.nsem += 1
        d = DmaSem(self.es.enter_context(self.nc.semaphore("d%d" % self.nsem)))
        self.dsems.append(d)
        return d

    def _deps(self, eng, reads, writes):
        need = {}

        def add(tok, same_ok):
            if tok is None:
                return
            sem, val, src = tok
            if same_ok and src is eng:
                return
            k = id(sem)
            if k not in need or need[k][1] < val:
                need[k] = (sem, val)

        for r in reads:
            add(r.w, False)
        for w in writes:
            add(w.w, True)
            for t in w.r.values():
                add(t, True)
        out = []
        for k, (sem, val) in need.items():
            if eng.known.get(k, 0) < val:
                eng.known[k] = val
                out.append((sem, val))
        return out

    def _mark(self, tok, reads, writes):
        for r in reads:
            k = id(tok[0])
            r.r[k] = tok
        for w in writes:
            w.w = tok
            w.r = {}

    def op(self, en, fn, reads=(), writes=()):
        eng = self.E[en]
        waits = self._deps(eng, reads, writes)
        eng.n += 1
        n = eng.n
        sem = eng.sem

        def run(e):
            for s, v in waits:
                e.wait_ge(s, v)
            fn(e).then_inc(sem, 1)

        eng.ins.append(run)
        self._mark((sem, n, eng), reads, writes)

    def group(self, fns, reads=(), writes=()):
        eng = self.E["pe"]
        waits = self._deps(eng, reads, writes)
        eng.n += 1
        sem = eng.sem

        def run(e):
            for s, v in waits:
                e.wait_ge(s, v)
            last = None
            for f in fns:
                last = f(e)
            last.then_inc(sem, 1)

        eng.ins.append(run)
        self._mark((sem, eng.n, eng), reads, writes)

    def dma(self, en, out, in_, ds, reads=(), writes=()):
        eng = self.E[en]
        waits = self._deps(eng, reads, writes)
        ds.n += 1
        val = ds.n * 16
        sem = ds.sem

        def run(e):
            for s, v in waits:
                e.wait_ge(s, v)
            e.dma_start(out=out, in_=in_).then_inc(sem, 16)

        eng.ins.append(run)
        self._mark((sem, val, None), reads, writes)

    def coll(self, ins, outs, ds, reads=(), writes=()):
        eng = self.E["pool"]
        waits = self._deps(eng, reads, writes)
        ds.n += 1
        assert ds.n == 1
        ds.unit = 1
        sem = ds.sem

        def run(e):
            for s, v in waits:
                e.wait_ge(s, v)
            e.collective_compute("AllGather", ALU.bypass,
                                 replica_groups=[[0, 1], [2, 3], [4, 5], [6, 7]],
                                 ins=ins, outs=outs).then_inc(sem)

        eng.ins.append(run)
        self._mark((sem, 1, None), reads, writes)

    SEM_ROTATE_AT = 1200

    def barrier(self):
        for eng in self.E.values():
            waits = []
            for o in self.E.values():
                if o.n == 0:
                    continue
                if o is eng and eng.name == "sp":
                    continue
                k = id(o.sem)
                if eng.known.get(k, 0) < o.n:
                    eng.known[k] = o.n
                    waits.append((o.sem, o.n))
            for d in self.dsems:
                k = id(d.sem)
                if d.n and eng.known.get(k, 0) < d.n * d.unit:
                    eng.known[k] = d.n * d.unit
                    waits.append((d.sem, d.n * d.unit))
            if waits:
                def run(e, waits=waits):
                    for s, v in waits:
                        e.wait_ge(s, v)
                eng.ins.append(run)
        for o in self.E.values():
            if o.n > self.SEM_ROTATE_AT:
                old = id(o.sem)
                for eng in self.E.values():
                    eng.known[old] = 1 << 40
                self.nsem += 1
                o.sem = self.es.enter_context(self.nc.semaphore("r%d_%s" % (self.nsem, o.name)))
                o.n = 0

    def final_wait(self, dsems):
        eng = self.E["sp"]
        waits = [(d.sem, d.n * d.unit) for d in dsems if d.n > 0]
        waits += [(o.sem, o.n) for o in self.E.values() if o is not eng and o.n > 0]

        def run(e):
            for s, v in waits:
                e.wait_ge(s, v)

        eng.ins.append(run)

    def emit(self, block):
        E = self.E

        @block.tensor
        def _(e):
            for f in E["pe"].ins:
                f(e)

        @block.scalar
        def _(e):
            for f in E["act"].ins:
                f(e)

        @block.vector
        def _(e):
            for f in E["dve"].ins:
                f(e)

        @block.gpsimd
        def _(e):
            for f in E["pool"].ins:
                f(e)

        @block.sync
        def _(e):
            for f in E["sp"].ins:
                f(e)


def build(stage=99, stop=None):
    from contextlib import ExitStack
    nc = bass.Bass("TRN2", target_bir_lowering=False)
    es = ExitStack()

    def din(name, shape):
        return nc.dram_tensor(name, list(shape), F32, kind="ExternalInput")

    def dout(name, shape):
        return nc.dram_tensor(name, list(shape), F32, kind="ExternalOutput")

    xT = din("xT", [D, NT])
    xh0 = din("xh0", [D, HALO])
    cs_in = din("cs_in", [2, D, HALO])
    par_d = din("par", [128, NPAR])
    flag_d = din("flag", [128, 1])
    mgen_d = din("mgen2", [128, 512])
    mfirst_d = din("mfirst2", [128, 512])
    mg2_d = din("mg2x8", [128, 512])
    msc_d = din("ms_cache", [128, 104])
    msn_d = din("ms_new", [128, 24])
    ckT = din("ckT", [2, 128, 8, 13 * 128])
    cv = din("cv", [2, 128, 13, 1024])
    w_pw1 = din("conv_w_pw1", [2, D, 2 * D])
    w_pw2 = din("conv_w_pw2", [2, D, D])
    w_qkv = din("attn_w_qkv", [2, D, 9216])
    w_o = din("attn_w_o", [2, 1024, D])
    w_1 = din("mlp_w1", [4, D, DFF])
    w_2 = din("mlp_w2", [4, DFF, D])

    yT = dout("yT", [D, NT])
    cs_p = dout("cs_p", [2, D, HALO])
    cs_s = dout("cs_s", [2, D, HALO])
    kT_out = dout("kT_out", [2, 3, 8, 128, NT])
    v_out = dout("v_out", [2, 3, NT, 1024])

    exk = [nc.dram_tensor("exk%d" % j, [128, KVX], BF16) for j in range(2)]
    exv = [nc.dram_tensor("exv%d" % j, [24 * NT, 128], BF16) for j in range(2)]
    NEED = [128, 512, 1024]
    sk = [[nc.dram_tensor("sk%d_%d" % (j, g), [128, 8 * NEED[g]], BF16) for g in range(3)] for j in range(2)]
    rk = [[nc.dram_tensor("rk%d_%d" % (j, g), [256, 8 * NEED[g]], BF16) for g in range(3)] for j in range(2)]
    sv = [[nc.dram_tensor("sv%d_%d" % (j, g), [128, 8 * NEED[g]], BF16) for g in range(3)] for j in range(2)]
    rv = [[nc.dram_tensor("rv%d_%d" % (j, g), [256, 8 * NEED[g]], BF16) for g in range(3)] for j in range(2)]
    xh_i = nc.dram_tensor("xh_i", [128, KC * HALO], F32)
    xh_o = nc.dram_tensor("xh_o", [256, KC * HALO], F32)

    sb = lambda name, shape, dt: es.enter_context(nc.sbuf_tensor(name, list(shape), dt))
    X = sb("X", [128, KC, NT], F32)
    H = sb("H", [128, KC, NH], BF16)
    W = [sb("W%d" % i, [128, KC, 256], BF16) for i in range(3)]
    PAR = sb("PAR", [128, NPAR], F32)
    ones = sb("ones", [128, 128], BF16)
    ident = sb("ident", [128, 128], BF16)
    identf = sb("identf", [128, 128], F32)
    epsr = sb("epsr", [128, 1], F32)
    epsl = sb("epsl", [128, 1], F32)
    flag = sb("flag_s", [128, 1], F32)
    Xh = sb("Xh", [128, KC, HALO], F32)
    CSI = sb("CSI", [128, KC, HALO], F32)
    ARENA_B = 77400
    AR = sb("AR", [128, ARENA_B // 2], BF16)
    ARf = AR.bitcast(F32)
    PS = [es.enter_context(nc.psum_tensor("ps%d" % i, [128, 512], F32)) for i in range(8)]

    def carve(off_bytes, cols, dt):
        if dt == F32:
            assert off_bytes % 4 == 0
            return ARf[:, off_bytes // 4: off_bytes // 4 + cols]
        assert off_bytes % 2 == 0
        return AR[:, off_bytes // 2: off_bytes // 2 + cols]

    p = Prog(nc, es)
    rX = [Res() for _ in range(KC)]
    rH = Res()
    rW = [Res() for _ in range(3)]
    dW = [p.dsem() for _ in range(3)]
    rPS = [Res() for _ in range(8)]
    rPAR = Res()
    rC = Res()
    dC = p.dsem()
    out_sems = []
    wctr = [0]
    pctr = [0]

    def next_w():
        i = wctr[0] % 3
        wctr[0] += 1
        return i

    def next_ps(lo=0, hi=6):
        i = lo + pctr[0] % (hi - lo)
        pctr[0] += 1
        return i

    def PARc(c, n=1):
        return PAR[:, c:c + n]

    p.dma("sp", PAR[:], par_d.ap(), dC, writes=[rPAR])
    p.dma("sp", flag[:], flag_d.ap(), dC, writes=[rC])
    xv = xT.ap().rearrange("(kc p) n -> p kc n", p=128)
    dX = p.dsem()
    for q in range(4):
        p.dma("sp", X[:, 4 * q:4 * q + 4, :], xv[:, 4 * q:4 * q + 4, :], dX, writes=rX[4 * q:4 * q + 4])
    rXh = Res()
    dXh = p.dsem()
    p.dma("sp", Xh[:], xh0.ap().rearrange("(kc p) n -> p kc n", p=128), dXh, writes=[rXh])
    p.op("pool", lambda e: e.memset(ones[:], 1.0), writes=[rC])
    p.op("pool", lambda e: e.memset(epsr[:], RMS_EPS), writes=[rC])
    p.op("pool", lambda e: e.memset(epsl[:], LN_EPS), writes=[rC])
    p.op("pool", lambda e: e.memset(identf[:], 0.0), writes=[rC])
    p.op("pool", lambda e: e.affine_select(out=identf[:], in_=identf[:], pattern=[[-1, 128]],
                                           compare_op=ALU.not_equal, fill=1.0, base=0, channel_multiplier=1),
         reads=[rC], writes=[rC])
    p.op("pool", lambda e: e.tensor_copy(out=ident[:], in_=identf[:]), reads=[rC], writes=[rC])

    def rmsnorm(gcol, with_halo, sq_off=0):
        SQ = carve(sq_off, KC * 354, BF16)
        rstd = carve(sq_off + KC * 354 * 2, NT, F32)
        sdt = carve(sq_off + KC * 354 * 2 + NT * 4, 354, F32)
        rSQ, rR, rS = Res(), Res(), Res()
        tl = [(X, c0, n, c0 + HALO, rX) for (c0, n) in TILES]
        if with_halo:
            tl.append((Xh, 0, HALO, 0, [rXh]))
        for (src, c0, n, h0, rsrc) in tl:
            sq = SQ.rearrange("p (k n) -> p k n", k=KC)[:, :, 0:n]
            p.op("act", lambda e, src=src, c0=c0, n=n, sq=sq: e.activation(out=sq, in_=src[:, :, c0:c0 + n], func=AF.Square),
                 reads=rsrc, writes=[rSQ])
            b = next_ps(6, 8)
            p.group([lambda e, kc=kc, b=b, n=n, sq=sq: e.matmul(PS[b][:, 0:n], lhsT=ones[:], rhs=sq[:, kc, :],
                                                               start=(kc == 0), stop=(kc == KC - 1), skip_group_check=True) for kc in range(KC)],
                    reads=[rSQ, rC], writes=[rPS[b]])
            p.op("act", lambda e, b=b, n=n: e.activation(out=sdt[:, 0:n], in_=PS[b][:, 0:n], func=AF.Sqrt,
                                                         bias=epsr[:], scale=1.0 / D),
                 reads=[rPS[b], rC], writes=[rS])
            rs = rstd[:, 0:n]
            p.op("dve", lambda e, n=n, rs=rs: e.reciprocal(out=rs, in_=sdt[:, 0:n]), reads=[rS], writes=[rR])
            for kc in range(KC):
                p.op("dve", lambda e, kc=kc, src=src, c0=c0, n=n, h0=h0, rs=rs: e.scalar_tensor_tensor(
                    out=H[:, kc, h0:h0 + n], in0=src[:, kc, c0:c0 + n], scalar=PARc(gcol + kc), in1=rs,
                    op0=ALU.mult, op1=ALU.mult),
                     reads=[rR, rPAR] + (rsrc if len(rsrc) == 1 else [rsrc[kc]]), writes=[rH])

    def load_stripe(wview, r0, nk, c0, ncols, extra=None):
        i = next_w()
        src = wview[r0:r0 + nk * 128, c0:c0 + ncols].rearrange("(kc p) n -> p kc n", p=128)
        p.dma("pool", W[i][:, 0:nk, 0:ncols], src, dW[i], writes=[rW[i]])
        if extra is not None:
            c1, off = extra
            src2 = wview[r0:r0 + nk * 128, c1:c1 + ncols].rearrange("(kc p) n -> p kc n", p=128)
            p.dma("pool", W[i][:, 0:nk, off:off + ncols], src2, dW[i], writes=[rW[i]])
        return i

    def mm_chunk(slot, col, nk, src, src_res, tiles, evac):
        banks = [next_ps() for _ in tiles]
        fns = []
        for kc in range(nk):
            for t, (c0, n) in enumerate(tiles):
                fns.append(lambda e, kc=kc, t=t, c0=c0, n=n: e.matmul(
                    PS[banks[t]][:, 0:n], lhsT=W[slot][:, kc, col:col + 128], rhs=src(kc, c0, n),
                    start=(kc == 0), stop=(kc == nk - 1), skip_group_check=True))
        p.group(fns, reads=[rW[slot]] + src_res, writes=[rPS[b] for b in banks])
        for t, (c0, n) in enumerate(tiles):
            evac(t, banks[t], c0, n)

    Hsrc = lambda kc, c0, n: H[:, kc, HALO + c0:HALO + c0 + n]

    def mlp(i):
        p.barrier()
        rmsnorm(P_NMLP + 16 * i, False, sq_off=KC * NT * 2)
        A = carve(0, KC * NT, BF16).rearrange("p (k n) -> p k n", k=KC)
        rA = [Res() for _ in range(KC)]
        rl = carve(KC * NT * 2 + 40000, 2 * 344, F32)
        rRL = [Res(), Res()]
        rc = [0]
        w1v = w_1.ap()[i]
        w2v = w_2.ap()[i]
        for q in range(4):
            for s in range(8):
                slot = load_stripe(w1v, 0, KC, q * 2048 + s * 256, 256)
                for oc in range(2):
                    ch = s * 2 + oc

                    def evac(t, b, c0, n, ch=ch):
                        k = rc[0] % 2
                        rc[0] += 1
                        tmp = rl[:, k * 344:k * 344 + n]
                        p.op("act", lambda e: e.activation(out=tmp, in_=PS[b][:, 0:n], func=AF.Relu),
                             reads=[rPS[b]], writes=[rRL[k]])
                        p.op("dve", lambda e: e.tensor_tensor(out=A[:, ch, c0:c0 + n], in0=tmp, in1=tmp, op=ALU.mult),
                             reads=[rRL[k]], writes=[rA[ch]])

                    mm_chunk(slot, oc * 128, KC, Hsrc, [rH], TILES, evac)
            for s in range(8):
                slot = load_stripe(w2v, q * 2048, KC, s * 256, 256)
                for oc in range(2):
                    o = s * 2 + oc

                    def evac(t, b, c0, n, o=o):
                        p.op("dve", lambda e: e.tensor_tensor(out=X[:, o, c0:c0 + n], in0=PS[b][:, 0:n],
                                                              in1=X[:, o, c0:c0 + n], op=ALU.add),
                             reads=[rPS[b], rX[o]], writes=[rX[o]])

                    mm_chunk(slot, oc * 128, KC, lambda kc, c0, n: A[:, kc, c0:c0 + n], rA, TILES, evac)

    def conv_layer(i, j):
        p.barrier()
        if i > 0:
            rXI, rXO = Res(), Res()
            dXI, dXC, dXL = p.dsem(), p.dsem(), p.dsem()
            p.dma("sp", xh_i.ap().rearrange("p (k n) -> p k n", k=KC), X[:, :, NP_ - HALO:NP_], dXI, reads=rX, writes=[rXI])
            p.coll([xh_i.ap().opt()], [xh_o.ap().opt()], dXC, reads=[rXI], writes=[rXO])
            p.dma("sp", Xh[:], xh_o.ap()[0:128, :].rearrange("p (k n) -> p k n", k=KC), dXL, reads=[rXO], writes=[rXh])
        pc = P_CONV + j * PC_SIZE
        C_OFF, G_OFF, SG_OFF = 0, KC * NT * 4, KC * NT * 4 + 4368
        rmsnorm(P_NMIX + 16 * i, True, sq_off=0)
        Cc = carve(C_OFF, KC * NT, F32).rearrange("p (k n) -> p k n", k=KC)
        G = carve(G_OFF, 1092, F32)
        SG = carve(SG_OFF, 2 * 354, F32)
        rG, rSG, rCc = Res(), [Res(), Res()], [Res() for _ in range(KC)]
        rCSI = Res()
        dCSI = p.dsem()
        p.dma("sp", CSI[:], cs_in.ap()[j].rearrange("(kc p) n -> p kc n", p=128), dCSI, writes=[rCSI])
        dO = p.dsem()
        out_sems.append(dO)
        csp_v = cs_p.ap()[j].rearrange("(kc p) n -> p kc n", p=128)
        css_v = cs_s.ap()[j].rearrange("(kc p) n -> p kc n", p=128)
        wv = w_pw1.ap()[j]
        sgc = [0]
        for kc in range(KC):
            if kc == 8:
                p.barrier()
            slot = load_stripe(wv, 0, KC, D + kc * 128, 128, extra=(kc * 128, 128))
            sig_t = {}

            def evac_g(t, b, c0, n, kc=kc):
                k = sgc[0] % 2
                sgc[0] += 1
                sig_t[t] = k
                p.op("act", lambda e: e.activation(out=SG[:, k * 354:k * 354 + n], in_=PS[b][:, 0:n], func=AF.Sigmoid,
                                                   bias=PARc(pc + PC_BG + kc), scale=1.0),
                     reads=[rPS[b], rPAR], writes=[rSG[k]])

            def evac_a(t, b, c0, n, kc=kc):
                k = sig_t[t]
                segs = [(c0, min(n, 1054 - c0), c0)] if c0 + n <= 1054 else [(c0, 1054 - c0, c0), (1054, 8, 1084)]
                for (h0, nn, g0) in segs:
                    o = h0 - c0
                    p.op("dve", lambda e, o=o, nn=nn, g0=g0: e.scalar_tensor_tensor(
                        out=G[:, g0:g0 + nn], in0=PS[b][:, o:o + nn], scalar=PARc(pc + PC_BA + kc),
                        in1=SG[:, k * 354 + o:k * 354 + o + nn], op0=ALU.add, op1=ALU.mult),
                         reads=[rPS[b], rSG[k], rPAR], writes=[rG])

            for t, (c0, n) in enumerate(CTILES):
                bg, ba = next_ps(), next_ps()
                fns = []
                for kk in range(KC):
                    fns.append(lambda e, kk=kk, c0=c0, n=n, bg=bg, slot=slot: e.matmul(PS[bg][:, 0:n], lhsT=W[slot][:, kk, 0:128],
                                                                         rhs=H[:, kk, c0:c0 + n], start=(kk == 0), stop=(kk == KC - 1), skip_group_check=True))
                for kk in range(KC):
                    fns.append(lambda e, kk=kk, c0=c0, n=n, ba=ba, slot=slot: e.matmul(PS[ba][:, 0:n], lhsT=W[slot][:, kk, 128:256],
                                                                         rhs=H[:, kk, c0:c0 + n], start=(kk == 0), stop=(kk == KC - 1), skip_group_check=True))
                p.group(fns, reads=[rW[slot], rH], writes=[rPS[bg], rPS[ba]])
                evac_g(t, bg, c0, n)
                evac_a(t, ba, c0, n)
            p.op("dve", lambda e: e.tensor_scalar(out=G[:, 0:HALO], in0=G[:, 0:HALO], scalar1=flag[:], scalar2=None, op0=ALU.mult),
                 reads=[rG, rC], writes=[rG])
            p.op("act", lambda e, kc=kc: e.copy(out=G[:, 1054:1084], in_=CSI[:, kc, :]), reads=[rCSI], writes=[rG])
            p.dma("sp", csp_v[:, kc, :], G[:, 1024:1054], dO, reads=[rG])
            p.dma("sp", css_v[:, kc, :], G[:, 1062:1092], dO, reads=[rG])
            wd = pc + PC_WDW + kc * 31
            for (o0, nn, g0) in ((0, NP_, 0), (NP_, NS, 1054)):
                p.op("dve", lambda e, o0=o0, nn=nn, g0=g0, kc=kc, wd=wd: e.tensor_scalar(
                    out=Cc[:, kc, o0:o0 + nn], in0=G[:, g0:g0 + nn], scalar1=PARc(wd), scalar2=PARc(pc + PC_BDW + kc),
                    op0=ALU.mult, op1=ALU.add), reads=[rG, rPAR], writes=[rCc[kc]])
                for tap in range(1, 31):
                    p.op("dve", lambda e, o0=o0, nn=nn, g0=g0, kc=kc, tap=tap, wd=wd: e.scalar_tensor_tensor(
                        out=Cc[:, kc, o0:o0 + nn], in0=G[:, g0 + tap:g0 + tap + nn], scalar=PARc(wd + tap),
                        in1=Cc[:, kc, o0:o0 + nn], op0=ALU.mult, op1=ALU.add), reads=[rG, rCc[kc]], writes=[rCc[kc]])
        p.barrier()
        T_OFF = G_OFF
        cb = carve(T_OFF, NT, BF16)
        cq = carve(T_OFF + NT * 2, NT, BF16)
        rcb, rcq = Res(), Res()
        bs = [next_ps() for _ in range(6)]
        for kc in range(KC):
            p.op("act", lambda e, kc=kc: e.activation(out=cb, in_=Cc[:, kc, :], func=AF.Copy), reads=[rCc[kc]], writes=[rcb])
            p.op("act", lambda e, kc=kc: e.activation(out=cq, in_=Cc[:, kc, :], func=AF.Square), reads=[rCc[kc]], writes=[rcq])
            fns = []
            for t, (c0, n) in enumerate(TILES):
                fns.append(lambda e, t=t, c0=c0, n=n, kc=kc: e.matmul(PS[bs[t]][:, 0:n], lhsT=ones[:], rhs=cb[:, c0:c0 + n],
                                                                     start=(kc == 0), stop=(kc == KC - 1), skip_group_check=True))
                fns.append(lambda e, t=t, c0=c0, n=n, kc=kc: e.matmul(PS[bs[3 + t]][:, 0:n], lhsT=ones[:], rhs=cq[:, c0:c0 + n],
                                                                     start=(kc == 0), stop=(kc == KC - 1), skip_group_check=True))
            p.group(fns, reads=[rcb, rcq, rC], writes=[rPS[b] for b in bs])
        p.barrier()
        mu = carve(T_OFF, NT, F32)
        R_OFF = SG_OFF + 2832
        rstd = carve(R_OFF, NT, F32)
        rmu, rrs = Res(), Res()
        for t, (c0, n) in enumerate(TILES):
            p.op("act", lambda e, t=t, c0=c0, n=n: e.activation(out=mu[:, c0:c0 + n], in_=PS[bs[t]][:, 0:n], func=AF.Identity, scale=1.0 / D),
                 reads=[rPS[bs[t]]], writes=[rmu])
            p.op("dve", lambda e, c0=c0, n=n: e.tensor_tensor(out=rstd[:, c0:c0 + n], in0=mu[:, c0:c0 + n], in1=mu[:, c0:c0 + n], op=ALU.mult),
                 reads=[rmu], writes=[rrs])
            p.op("dve", lambda e, t=t, c0=c0, n=n: e.scalar_tensor_tensor(out=rstd[:, c0:c0 + n], in0=PS[bs[3 + t]][:, 0:n], scalar=1.0 / D,
                                                                       in1=rstd[:, c0:c0 + n], op0=ALU.mult, op1=ALU.subtract),
                 reads=[rPS[bs[3 + t]], rrs], writes=[rrs])
            p.op("act", lambda e, c0=c0, n=n: e.activation(out=rstd[:, c0:c0 + n], in_=rstd[:, c0:c0 + n], func=AF.Sqrt, bias=epsl[:], scale=1.0),
                 reads=[rrs, rC], writes=[rrs])
            p.op("dve", lambda e, c0=c0, n=n: e.reciprocal(out=rstd[:, c0:c0 + n], in_=rstd[:, c0:c0 + n]), reads=[rrs], writes=[rrs])
        for kc in range(KC):
            p.op("dve", lambda e, kc=kc: e.tensor_tensor(out=Cc[:, kc, :], in0=Cc[:, kc, :], in1=mu[:, :], op=ALU.subtract),
                 reads=[rCc[kc], rmu], writes=[rCc[kc]])
            p.op("pool", lambda e, kc=kc: e.tensor_tensor(out=Cc[:, kc, :], in0=Cc[:, kc, :], in1=rstd[:, :], op=ALU.mult),
                 reads=[rCc[kc], rrs], writes=[rCc[kc]])
            p.op("act", lambda e, kc=kc: e.activation(out=H[:, kc, HALO:NH], in_=Cc[:, kc, :], func=AF.Silu,
                                                      bias=PARc(pc + PC_LNB + kc), scale=PARc(pc + PC_LNG + kc)),
                 reads=[rCc[kc], rPAR], writes=[rH])
        w2v = w_pw2.ap()[j]
        for s in range(8):
            slot = load_stripe(w2v, 0, KC, s * 256, 256)
            for oc in range(2):
                o = s * 2 + oc

                def evac(t, b, c0, n, o=o):
                    p.op("dve", lambda e: e.scalar_tensor_tensor(out=X[:, o, c0:c0 + n], in0=PS[b][:, 0:n],
                                                                 scalar=PARc(pc + PC_BPW2 + o), in1=X[:, o, c0:c0 + n],
                                                                 op0=ALU.add, op1=ALU.add),
                         reads=[rPS[b], rX[o], rPAR], writes=[rX[o]])

                mm_chunk(slot, oc * 128, KC, Hsrc, [rH], TILES, evac)


    def mm(out, lhsT, rhs, start, stop):
        return lambda e: e.matmul(out, lhsT=lhsT, rhs=rhs, start=start, stop=stop, skip_group_check=True)

    def attn_layer(i, j):
        p.barrier()
        A0 = 3328
        OT = carve(A0, 8 * NT, BF16).rearrange("p (h n) -> p h n", h=8)
        B0 = A0 + 8 * NT * 2
        QT = carve(B0, 6 * NT, BF16).rearrange("p (a g n) -> p a g n", a=2, g=3)
        C0 = B0 + 6 * NT * 2
        rmsnorm(P_NMIX + 16 * i, False, sq_off=C0)
        p.barrier()
        mgen, mfirst, mg2 = carve(0, 512, BF16), carve(1024, 512, BF16), carve(2048, 512, BF16)
        msc, msn = carve(3072, 104, BF16), carve(3280, 24, BF16)
        rM = Res()
        dM = p.dsem()
        for dst, src in ((mgen, mgen_d), (mfirst, mfirst_d), (mg2, mg2_d), (msc, msc_d)):
            p.dma("pool", dst, src.ap(), dM, writes=[rM])
        p.dma("pool", msn, msn_d.ap(), dM, writes=[rM])
        wq = w_qkv.ap()[j]
        if stop == "A0":
            return
        KF = [carve(C0 + k * 4128, NT, F32) for k in range(2)]
        KB = [carve(C0 + 8256 + k * 2064, NT, BF16) for k in range(2)]
        VF = [carve(C0 + 12384 + k * 1024, 256, F32) for k in range(2)]
        VB = [carve(C0 + 14432 + k * 512, 256, BF16) for k in range(2)]
        rKF, rKB, rVF, rVB = [Res(), Res()], [Res(), Res()], [Res(), Res()], [Res(), Res()]
        dKF, dKB, dVF, dVB = [p.dsem(), p.dsem()], [p.dsem(), p.dsem()], [p.dsem(), p.dsem()], [p.dsem(), p.dsem()]
        out_sems.extend(dKF + dVF)
        rEK, rEV = Res(), Res()
        rSK, rSV = [Res() for _ in range(3)], [Res() for _ in range(3)]
        exk_v = exk[j].ap().rearrange("p (g h t) -> p g h t", g=3, h=8)
        exv_v = exv[j].ap().rearrange("(g h t) e -> g h t e", g=3, h=8)
        cnt = [0, 0]
        for g in range(3):
            for hp in range(4):
                slot = load_stripe(wq, 0, KC, g * 3072 + 1024 + hp * 256, 256)
                for oc in range(2):
                    h = hp * 2 + oc
                    k = cnt[0] % 2
                    cnt[0] += 1

                    def evac(t, b, c0, n, k=k):
                        p.op("act", lambda e: e.activation(out=KF[k][:, c0:c0 + n], in_=PS[b][:, 0:n], func=AF.Copy),
                             reads=[rPS[b]], writes=[rKF[k]])
                        p.op("dve", lambda e: e.tensor_copy(out=KB[k][:, c0:c0 + n], in_=KF[k][:, c0:c0 + n]),
                             reads=[rKF[k]], writes=[rKB[k]])

                    mm_chunk(slot, oc * 128, KC, Hsrc, [rH], TILES, evac)
                    p.dma("sp", kT_out.ap()[j, g, h], KF[k][:, :], dKF[k], reads=[rKF[k]])
                    p.dma("sp", exk_v[:, g, h, :], KB[k][:, :], dKB[k], reads=[rKB[k]], writes=[rEK])
                    nd = NEED[g]
                    p.dma("sp", sk[j][g].ap()[:, h * nd:(h + 1) * nd], KB[k][:, NP_ - nd:NP_], dKB[k], reads=[rKB[k]], writes=[rSK[g]])
                if stop == "AK":
                    continue
                slot = load_stripe(wq, 0, KC, g * 3072 + 2048 + hp * 256, 256)
                for tb in range(9 if stop != "AV8" else 8):
                    M = 128 if tb < 8 else NS
                    t0 = tb * 128
                    b = next_ps()
                    p.group([mm(PS[b][0:M, 0:256], H[:, kc, HALO + t0:HALO + t0 + M], W[slot][:, kc, 0:256], kc == 0, kc == KC - 1)
                             for kc in range(KC)], reads=[rW[slot], rH], writes=[rPS[b]])
                    k = cnt[1] % 2
                    cnt[1] += 1
                    p.op("act", lambda e, k=k, b=b, M=M: e.activation(out=VF[k][0:M, :], in_=PS[b][0:M, 0:256], func=AF.Copy),
                         reads=[rPS[b]], writes=[rVF[k]])
                    p.op("dve", lambda e, k=k, M=M: e.tensor_copy(out=VB[k][0:M, :], in_=VF[k][0:M, :]),
                         reads=[rVF[k]], writes=[rVB[k]])
                    p.dma("sp", v_out.ap()[j, g, t0:t0 + M, hp * 256:hp * 256 + 256], VF[k][0:M, :], dVF[k], reads=[rVF[k]])
                    dst = exv_v[g, hp * 2:hp * 2 + 2, t0:t0 + M, :].rearrange("h t e -> t h e")
                    p.dma("sp", dst, VB[k][0:M, :].rearrange("t (h e) -> t h e", h=2), dVB[k], reads=[rVB[k]], writes=[rEV])
                    nd = NEED[g]
                    if tb < 8 and t0 >= NP_ - nd:
                        tl_ = t0 - (NP_ - nd)
                        sva = sv[j][g].ap()
                        dst2 = bass.AP(sva.tensor, sva.offset + ((hp * 2) * nd + tl_) * 128, [[128, M], [nd * 128, 2], [1, 128]])
                        p.dma("sp", dst2, VB[k][0:M, :].rearrange("t (h e) -> t h e", h=2), dVB[k], reads=[rVB[k]], writes=[rSV[g]])
        if stop in ("A", "AK", "AV8"):
            return
        rRK, rRV = [Res() for _ in range(3)], [Res() for _ in range(3)]
        for g in range(3):
            p.coll([sk[j][g].ap().opt()], [rk[j][g].ap().opt()], p.dsem(), reads=[rSK[g]], writes=[rRK[g]])
            p.coll([sv[j][g].ap().opt()], [rv[j][g].ap().opt()], p.dsem(), reads=[rSV[g]], writes=[rRV[g]])
        p.barrier()
        if stop == "B":
            return
        o = [C0]

        def take(cols, dt):
            ap = carve(o[0], cols, dt)
            o[0] += cols * (4 if dt == F32 else 2)
            return ap

        ownK = take(3 * NT, BF16).rearrange("p (g n) -> p g n", g=3)
        pK0, pK1, pK2 = take(128, BF16), take(512, BF16), take(1024, BF16)
        Vg0 = take(8 * 128, BF16).rearrange("p (b e) -> p b e", b=8)
        Vp0 = take(128, BF16)
        Vg1 = take(8 * 128, BF16).rearrange("p (b r e) -> p b r e", b=2, r=4)
        Vp1 = take(4 * 128, BF16).rearrange("p (r e) -> p r e", r=4)
        Vg2 = take(16 * 128, BF16).rearrange("p (r e) -> p r e", r=16)
        Vs = take(3 * 128, BF16).rearrange("p (g e) -> p g e", g=3)
        cK = take(13 * 128, BF16)
        cV = take(13 * 128, BF16).rearrange("p (c e) -> p c e", c=13)
        accO, accD = take(NT, F32), take(NT, F32)
        Pt = [take(512, BF16) for _ in range(2)]
        Pc, Pn = take(104, BF16), take(24, BF16)
        assert o[0] <= ARENA_B, o[0]
        rHDl = [Res() for _ in range(13)]
        dHD = [p.dsem() for _ in range(4)]
        rQT = [Res(), Res()]
        rOT = Res()
        racc = Res()
        rPt = [Res(), Res()]
        rPc = Res()
        SB_, OB_, DB_ = (0, 1), (2, 3), (4, 5)
        uc = [0, 0]
        ckv = ckT.ap()[j]
        cvv = cv.ap()[j]
        ev = exv[j].ap()

        def rows(base_ap, r0, pstep, np_, dims):
            return bass.AP(base_ap.tensor, base_ap.offset + r0 * 128, [[pstep * 128, np_]] + [[st * 128, c] for (st, c) in dims] + [[1, 128]])

        for hp in range(4):
            for g in range(3):
                slot = load_stripe(wq, 0, KC, g * 3072 + hp * 256, 256)
                for oc in range(2):
                    def evac(t, b, c0, n, oc=oc, g=g):
                        p.op("act", lambda e: e.activation(out=QT[:, oc, g, c0:c0 + n], in_=PS[b][:, 0:n], func=AF.Copy),
                             reads=[rPS[b]], writes=[rQT[oc]])
                    mm_chunk(slot, oc * 128, KC, Hsrc, [rH], TILES, evac)
            if stop == "Q":
                return
            for hh in range(2):
                h = hp * 2 + hh
                if stop in ("L", "G0", "G1", "G2") and h > 0:
                    return
                qt = lambda g, sl, hh=hh: QT[:, hh, g, sl]
                p.dma("sp", ownK, exk_v[:, :, h, :], dHD[0], reads=[rEK], writes=[rHDl[0]])
                p.dma("sp", pK0, rk[j][0].ap()[0:128, h * 128:(h + 1) * 128], dHD[0], reads=[rRK[0]], writes=[rHDl[1]])
                p.dma("sp", pK1, rk[j][1].ap()[0:128, h * 512:(h + 1) * 512], dHD[0], reads=[rRK[1]], writes=[rHDl[2]])
                p.dma("sp", pK2, rk[j][2].ap()[0:128, h * 1024:(h + 1) * 1024], dHD[0], reads=[rRK[2]], writes=[rHDl[3]])
                r_g = lambda g, h=h: (g * 8 + h) * NT
                p.dma("sp", Vg0, rows(ev, r_g(0), 1, 128, [(128, 8)]), dHD[1], reads=[rEV], writes=[rHDl[4]])
                p.dma("sp", Vp0, rows(rv[j][0].ap(), h * 128, 1, 128, []), dHD[1], reads=[rRV[0]], writes=[rHDl[5]])
                for b_ in range(2):
                    p.dma("sp", Vg1[:, b_], rows(ev, r_g(1) + 512 * b_, 4, 128, [(1, 4)]), dHD[1], reads=[rEV], writes=[rHDl[6]])
                p.dma("sp", Vp1, rows(rv[j][1].ap(), h * 512, 4, 128, [(1, 4)]), dHD[1], reads=[rRV[1]], writes=[rHDl[7]])
                p.dma("sp", Vg2[64:128], rows(ev, r_g(2), 16, 64, [(1, 16)]), dHD[2], reads=[rEV], writes=[rHDl[8]])
                p.dma("sp", Vg2[0:64], rows(rv[j][2].ap(), h * 1024, 16, 64, [(1, 16)]), dHD[2], reads=[rRV[2]], writes=[rHDl[9]])
                p.dma("sp", Vs[0:8], rows(ev, r_g(0) + NP_, 1, 8, [(8 * NT, 3)]), dHD[2], reads=[rEV], writes=[rHDl[10]])
                p.dma("pool", cK, ckv[:, h, :], dHD[3], writes=[rHDl[11]])
                p.dma("pool", cV, cvv[:, :, h * 128:(h + 1) * 128], dHD[3], writes=[rHDl[12]])

                def s_tile(mask, smm):
                    sb_ = SB_[uc[0] % 2]
                    k = uc[0] % 2
                    uc[0] += 1
                    fns = [mm(PS[sb_][:, 0:512], ident[:], mask, True, False)]
                    lst = smm(PS[sb_])
                    for idx, (out_, l_, r_) in enumerate(lst):
                        fns.append(mm(out_, l_, r_, False, idx == len(lst) - 1))
                    p.group(fns, reads=rHDl + [rM, rC, rQT[hh]], writes=[rPS[sb_]])
                    p.op("act", lambda e: e.activation(out=Pt[k][:, :], in_=PS[sb_][:, 0:512], func=AF.Exp, scale=SCALE),
                         reads=[rPS[sb_]], writes=[rPt[k]])
                    return k

                def od_banks():
                    ob, db = OB_[uc[1] % 2], DB_[uc[1] % 2]
                    uc[1] += 1
                    return ob, db

                def accum(ob, db, view, first):
                    for bank, acc in ((ob, accO), (db, accD)):
                        dst = view(acc)
                        src = PS[bank][:, 0:512]
                        if len(dst.shape) == 3:
                            src = src.rearrange("p (a b) -> p a b", a=dst.shape[1])
                        elif len(dst.shape) == 4:
                            src = src.rearrange("p (a b c) -> p a b c", a=dst.shape[1], b=dst.shape[2])
                        if first:
                            p.op("dve", lambda e, dst=dst, src=src: e.tensor_copy(out=dst, in_=src), reads=[rPS[bank]], writes=[racc])
                        else:
                            p.op("dve", lambda e, dst=dst, src=src: e.tensor_tensor(out=dst, in0=src, in1=dst, op=ALU.add),
                                 reads=[rPS[bank], racc], writes=[racc])

                if stop == "L":
                    continue
                for n0 in (0, 4):
                    ob, db = od_banks()
                    pv, dn = [], []
                    for n1 in (n0, n0 + 2):
                        def smm(S, n1=n1):
                            r = []
                            for u in range(2):
                                n = n1 + u
                                q = qt(0, slice(n * 128, n * 128 + 128))
                                kp = pK0[:, :] if n == 0 else ownK[:, 0, (n - 1) * 128:n * 128]
                                r.append((S[:, u * 256:u * 256 + 128], kp, q))
                                r.append((S[:, u * 256 + 128:u * 256 + 256], ownK[:, 0, n * 128:n * 128 + 128], q))
                            return r
                        k = s_tile(mfirst if n1 == 0 else mgen, smm)
                        for u in range(2):
                            n = n1 + u
                            oc_ = (n - n0) * 128
                            vp = Vp0[:, :] if n == 0 else Vg0[:, n - 1, :]
                            pv.append((k, mm(PS[ob][:, oc_:oc_ + 128], vp, Pt[k][:, u * 256:u * 256 + 128], True, False)))
                            pv.append((k, mm(PS[ob][:, oc_:oc_ + 128], Vg0[:, n, :], Pt[k][:, u * 256 + 128:u * 256 + 256], False, True)))
                            dn.append((k, mm(PS[db][:, oc_:oc_ + 128], ones[:], Pt[k][:, u * 256:u * 256 + 128], True, False)))
                            dn.append((k, mm(PS[db][:, oc_:oc_ + 128], ones[:], Pt[k][:, u * 256 + 128:u * 256 + 256], False, True)))
                    p.group([f for _, f in pv], reads=[rPt[0], rPt[1]] + rHDl, writes=[rPS[ob]])
                    p.group([f for _, f in dn], reads=[rPt[0], rPt[1], rC], writes=[rPS[db]])
                    accum(ob, db, lambda acc, n0=n0: acc[:, n0 * 128:n0 * 128 + 512], True)
                if stop == "G0":
                    continue
                for r0 in (0, 2):
                    ob, db = od_banks()
                    pv, dn = [], []
                    for r in (r0, r0 + 1):
                        def smm(S, r=r):
                            res_ = []
                            for b_ in range(2):
                                q = qt(1, slice(512 * b_ + r, 512 * b_ + 512, 4))
                                kp = pK1[:, r:512:4] if b_ == 0 else ownK[:, 1, r:512:4]
                                res_.append((S[:, b_ * 256:b_ * 256 + 128], kp, q))
                                res_.append((S[:, b_ * 256 + 128:b_ * 256 + 256], ownK[:, 1, 512 * b_ + r:512 * b_ + 512:4], q))
                            return res_
                        k = s_tile(mfirst, smm)
                        for b_ in range(2):
                            oc_ = ((r - r0) * 2 + b_) * 128
                            vp = Vp1[:, r, :] if b_ == 0 else Vg1[:, 0, r, :]
                            pv.append(mm(PS[ob][:, oc_:oc_ + 128], vp, Pt[k][:, b_ * 256:b_ * 256 + 128], True, False))
                            pv.append(mm(PS[ob][:, oc_:oc_ + 128], Vg1[:, b_, r, :], Pt[k][:, b_ * 256 + 128:b_ * 256 + 256], False, True))
                            dn.append(mm(PS[db][:, oc_:oc_ + 128], ones[:], Pt[k][:, b_ * 256:b_ * 256 + 128], True, False))
                            dn.append(mm(PS[db][:, oc_:oc_ + 128], ones[:], Pt[k][:, b_ * 256 + 128:b_ * 256 + 256], False, True))
                    p.group(pv, reads=[rPt[0], rPt[1]] + rHDl, writes=[rPS[ob]])
                    p.group(dn, reads=[rPt[0], rPt[1], rC], writes=[rPS[db]])
                    accum(ob, db, lambda acc, r0=r0: acc[:, 0:NP_].rearrange("p (b i f) -> p f b i", b=2, f=4)[:, r0:r0 + 2, :, :], False)
                if stop == "G1":
                    continue
                for half in range(2):
                    ob, db = od_banks()

                    def smm(S, half=half):
                        res_ = []
                        for c in range(8):
                            r = 8 * half + c
                            q = qt(2, slice(r, NP_, 16))
                            res_.append((S[0:64, c * 64:c * 64 + 64], pK2[:, r:NP_:16], q))
                            res_.append((S[64:128, c * 64:c * 64 + 64], ownK[:, 2, r:NP_:16], q))
                        return res_
                    k = s_tile(mg2, smm)
                    p.group([mm(PS[ob][:, c * 64:c * 64 + 64], Vg2[:, 8 * half + c, :], Pt[k][:, c * 64:c * 64 + 64], True, True) for c in range(8)],
                            reads=[rPt[k]] + rHDl, writes=[rPS[ob]])
                    p.group([mm(PS[db][:, 0:512], ones[:], Pt[k][:, :], True, True)], reads=[rPt[k], rC], writes=[rPS[db]])
                    accum(ob, db, lambda acc, half=half: acc[:, 0:NP_].rearrange("p (i r) -> p r i", r=16)[:, 8 * half:8 * half + 8, :], False)
                if stop == "G2":
                    continue
                grp = [0] + [1] * 4 + [2] * 8
                fns = [mm(PS[6][:, 0:104], ident[:], msc, True, False), mm(PS[6][0:8, 104:128], ident[0:8, 0:8], msn[0:8, :], False, False)]
                for c in range(13):
                    fns.append(mm(PS[6][:, c * 8:c * 8 + 8], cK[:, c * 128:c * 128 + 128], qt(grp[c], slice(NP_, NT)), False, False))
                for g in range(3):
                    fns.append(mm(PS[6][0:8, 104 + g * 8:112 + g * 8], ownK[:, g, NP_:NT], qt(g, slice(NP_, NT)), False, g == 2))
                p.group(fns, reads=rHDl + [rM, rC, rQT[hh]], writes=[rPS[6]])
                p.op("act", lambda e: e.activation(out=Pc, in_=PS[6][:, 0:104], func=AF.Exp, scale=SCALE), reads=[rPS[6]], writes=[rPc])
                p.op("act", lambda e: e.activation(out=Pn[0:8, :], in_=PS[6][0:8, 104:128], func=AF.Exp, scale=SCALE), reads=[rPS[6]], writes=[rPc])
                fo, fd = [], []
                for c in range(13):
                    fo.append(mm(PS[7][:, 0:8], cV[:, c, :], Pc[:, c * 8:c * 8 + 8], c == 0, False))
                    fd.append(mm(PS[7][:, 8:16], ones[:], Pc[:, c * 8:c * 8 + 8], c == 0, False))
                for g in range(3):
                    fo.append(mm(PS[7][:, 0:8], Vs[0:8, g, :], Pn[0:8, g * 8:g * 8 + 8], False, g == 2))
                    fd.append(mm(PS[7][:, 8:16], ones[0:8, :], Pn[0:8, g * 8:g * 8 + 8], False, g == 2))
                p.group(fo + fd, reads=[rPc, rC] + rHDl, writes=[rPS[7]])
                p.op("dve", lambda e: e.tensor_copy(out=accO[:, NP_:NT], in_=PS[7][:, 0:8]), reads=[rPS[7]], writes=[racc])
                p.op("dve", lambda e: e.tensor_copy(out=accD[:, NP_:NT], in_=PS[7][:, 8:16]), reads=[rPS[7]], writes=[racc])
                p.op("dve", lambda e: e.reciprocal(out=accD, in_=accD), reads=[racc], writes=[racc])
                p.op("dve", lambda e, h=h: e.tensor_tensor(out=OT[:, h, :], in0=accO, in1=accD, op=ALU.mult), reads=[racc], writes=[rOT])
        wov = w_o.ap()[j]
        for s_ in range(8):
            slot = load_stripe(wov, 0, 8, s_ * 256, 256)
            for oc in range(2):
                oo = s_ * 2 + oc

                def evac(t, b, c0, n, oo=oo):
                    p.op("dve", lambda e: e.tensor_tensor(out=X[:, oo, c0:c0 + n], in0=PS[b][:, 0:n], in1=X[:, oo, c0:c0 + n], op=ALU.add),
                         reads=[rPS[b], rX[oo]], writes=[rX[oo]])

                mm_chunk(slot, oc * 128, 8, lambda kc, c0, n: OT[:, kc, c0:c0 + n], [rOT], TILES, evac)

    for i in range(4):
        j = i // 2
        if stop is not None and i >= 2:
            break
        if i % 2 == 0:
            if stage >= 2:
                conv_layer(i, j)
        else:
            if stage >= 3:
                attn_layer(i, j)
        if stage >= 1 and not (stop is not None and i == 1):
            mlp(i)
    p.barrier()
    YF = carve(0, KC * NT, F32).rearrange("p (k n) -> p k n", k=KC)
    rstd_f = carve(KC * NT * 4, NT, F32)
    rY = [Res() for _ in range(KC)]
    dY = p.dsem()
    out_sems.append(dY)
    yv = yT.ap().rearrange("(kc p) n -> p kc n", p=128)
    for (c0, n) in TILES:
        sq = H[:, :, 0:n]
        rq, rr = Res(), Res()
        p.op("act", lambda e, c0=c0, n=n, sq=sq: e.activation(out=sq, in_=X[:, :, c0:c0 + n], func=AF.Square), reads=rX, writes=[rq])
        b = next_ps(6, 8)
        p.group([lambda e, kc=kc, b=b, n=n, sq=sq: e.matmul(PS[b][:, 0:n], lhsT=ones[:], rhs=sq[:, kc, :],
                                                           start=(kc == 0), stop=(kc == KC - 1), skip_group_check=True) for kc in range(KC)],
                reads=[rq, rC], writes=[rPS[b]])
        p.op("act", lambda e, b=b, n=n, c0=c0: e.activation(out=rstd_f[:, c0:c0 + n], in_=PS[b][:, 0:n], func=AF.Sqrt, bias=epsr[:], scale=1.0 / D),
             reads=[rPS[b], rC], writes=[rr])
        p.op("dve", lambda e, n=n, c0=c0: e.reciprocal(out=rstd_f[:, c0:c0 + n], in_=rstd_f[:, c0:c0 + n]), reads=[rr], writes=[rr])
        for kc in range(KC):
            p.op("dve", lambda e, kc=kc, c0=c0, n=n: e.scalar_tensor_tensor(
                out=YF[:, kc, c0:c0 + n], in0=X[:, kc, c0:c0 + n], scalar=PARc(P_NFIN + kc), in1=rstd_f[:, c0:c0 + n],
                op0=ALU.mult, op1=ALU.mult), reads=[rr, rPAR, rX[kc]], writes=[rY[kc]])
    for q in range(4):
        p.dma("sp", yv[:, 4 * q:4 * q + 4, :], YF[:, 4 * q:4 * q + 4, :], dY, reads=rY[4 * q:4 * q + 4])
    p.final_wait(out_sems)
    with nc.Block() as block:
        p.emit(block)
    es.close()
    return nc


def _fm(v):
    return np.ascontiguousarray(np.moveaxis(v.reshape(v.shape[:-1] + (KC, 128)), -1, 0))


def _pack_params(inp):
    par = np.zeros((128, NPAR), np.float32)
    par[:, P_NMIX:P_NMIX + 64] = _fm(inp["norm_mix"]).reshape(128, 64)
    par[:, P_NMLP:P_NMLP + 64] = _fm(inp["norm_mlp"]).reshape(128, 64)
    par[:, P_NFIN:P_NFIN + 16] = _fm(inp["norm_final"])
    for j in range(2):
        pc = P_CONV + j * PC_SIZE
        b1 = inp["conv_b_pw1"][j]
        par[:, pc + PC_BA:pc + PC_BA + 16] = _fm(b1[:D])
        par[:, pc + PC_BG:pc + PC_BG + 16] = _fm(b1[D:])
        wd = _fm(inp["conv_w_dw"][j])
        par[:, pc + PC_WDW:pc + PC_WDW + 496] = np.transpose(wd, (0, 2, 1)).reshape(128, 496)
        par[:, pc + PC_BDW:pc + PC_BDW + 16] = _fm(inp["conv_b_dw"][j])
        par[:, pc + PC_LNG:pc + PC_LNG + 16] = _fm(inp["conv_ln_g"][j])
        par[:, pc + PC_LNB:pc + PC_LNB + 16] = _fm(inp["conv_ln_b"][j])
        par[:, pc + PC_BPW2:pc + PC_BPW2 + 16] = _fm(inp["conv_b_pw2"][j])
    return par


def _masks(role):
    k = np.arange(128)[:, None]
    q = np.arange(128)[None, :]
    prev_ok = (q <= k)
    own_ok = (k <= q)
    gen = np.concatenate([prev_ok, own_ok], 1)
    first = np.concatenate([prev_ok if role == 1 else np.zeros_like(prev_ok), own_ok], 1)
    m = lambda ok: np.where(ok, 0.0, NEG).astype(np.float32)
    mgen2 = m(np.concatenate([gen, gen], 1))
    mfirst2 = m(np.concatenate([first, gen], 1))
    kk = np.arange(128)[:, None]
    qq = np.arange(64)[None, :]
    g2 = np.where(kk < 64, (role == 1) & (qq >= 0), (kk - 64) <= qq)
    mg2x8 = m(np.tile(g2, (1, 8)))
    i = np.arange(128)[:, None]
    s = np.arange(8)[None, :]
    cols = [i >= s]
    for r in range(4):
        cols.append(((s % 4) == r) & (i >= s // 4))
    for r in range(8):
        cols.append((s == r) & (i >= 0))
    ms_cache = m(np.concatenate(cols, 1))
    jn = np.arange(8)[:, None]
    ms_new = m(np.concatenate([jn <= s, (jn <= s) & ((s - jn) % 4 == 0), jn == s], 1))
    ms_new = np.concatenate([ms_new, np.zeros((120, 24), np.float32)], 0)
    return mgen2, mfirst2, mg2x8, ms_cache, ms_new


def _prep_inputs(inp):
    x_p, x_s = inp["x_prompt"], inp["x_sample"]
    par = _pack_params(inp)
    caches = [inp["cache_kv_w128"], inp["cache_kv_w512"], inp["cache_kv_w2048"]]
    dil = [1, 4, 16]
    maps = []
    for c in range(8):
        b, role = c // 2, c % 2
        t0 = role * NP_
        xT = np.concatenate([x_p[b, t0:t0 + NP_].T, x_s[c].T], 1)
        xh0 = x_p[b, t0 - HALO:t0].T if role == 1 else np.zeros((D, HALO), np.float32)
        cs_in = np.transpose(inp["state_conv"][:, c], (0, 2, 1))
        ckT = np.zeros((2, 128, 8, 13 * 128), np.float32)
        cv = np.zeros((2, 128, 13, 1024), np.float32)
        cls = 0
        for g in range(3):
            d = dil[g]
            ncls = [1, 4, 8][g]
            kv = caches[g][:, c]
            for r in range(ncls):
                rows = kv[:, r::d][:, :128]
                ckT[:, :, :, cls * 128:(cls + 1) * 128] = np.transpose(rows[:, :, 0], (0, 3, 2, 1))
                cv[:, :, cls, :] = rows[:, :, 1].reshape(2, 128, 1024)
                cls += 1
        mgen2, mfirst2, mg2x8, ms_cache, ms_new = _masks(role)
        m = {
            "xT": np.ascontiguousarray(xT), "xh0": np.ascontiguousarray(xh0), "cs_in": np.ascontiguousarray(cs_in),
            "par": par, "flag": np.full((128, 1), float(role), np.float32),
            "mgen2": mgen2, "mfirst2": mfirst2, "mg2x8": mg2x8, "ms_cache": ms_cache, "ms_new": ms_new,
            "ckT": ckT, "cv": cv,
        }
        for k in ("conv_w_pw1", "conv_w_pw2", "attn_w_qkv", "attn_w_o", "mlp_w1", "mlp_w2"):
            m[k] = inp[k]
        maps.append(m)
    return maps


def _assemble(res):
    y_p = np.zeros((4, 2048, D), np.float32)
    y_s = np.zeros((8, NS, D), np.float32)
    csp = np.zeros((2, 4, HALO, D), np.float32)
    css = np.zeros((2, 8, HALO, D), np.float32)
    wins = [128, 512, 2048]
    kvp = [np.zeros((2, 4, w, 2, 8, 128), np.float32) for w in wins]
    kvs = [np.zeros((2, 8, NS, 2, 8, 128), np.float32) for _ in wins]
    for c in range(8):
        r = res[c]
        b, role = c // 2, c % 2
        t0 = role * NP_
        yT = r["yT"]
        y_p[b, t0:t0 + NP_] = yT[:, :NP_].T
        y_s[c] = yT[:, NP_:].T
        css[:, c] = np.transpose(r["cs_s"], (0, 2, 1))
        if role == 1:
            csp[:, b] = np.transpose(r["cs_p"], (0, 2, 1))
        kT = r["kT_out"]
        vo = r["v_out"]
        for g in range(3):
            kk = np.transpose(kT[:, g], (0, 3, 1, 2))
            vv = vo[:, g].reshape(2, NT, 8, 128)
            kvs[g][:, c, :, 0] = kk[:, NP_:]
            kvs[g][:, c, :, 1] = vv[:, NP_:]
            w = wins[g]
            lo = max(2048 - w, t0)
            hi = t0 + NP_
            if lo < hi:
                dst = slice(lo - (2048 - w), hi - (2048 - w))
                kvp[g][:, b, dst, 0] = kk[:, lo - t0:hi - t0]
                kvp[g][:, b, dst, 1] = vv[:, lo - t0:hi - t0]
    return (y_p, y_s, csp, css, kvp[0], kvs[0], kvp[1], kvs[1], kvp[2], kvs[2])


_NC_CACHE = {}


def kernel(**inputs):
    inp = {k: np.asarray(v) for k, v in inputs.items()}
    if "nc" not in _NC_CACHE:
        _NC_CACHE["nc"] = build()
    maps = _prep_inputs(inp)
    res = run_bass_kernel_spmd(_NC_CACHE["nc"], maps, core_ids=list(range(8)))
    return _assemble(res.results)
```
